# Optimizing a Trainium2 kernel written in Bass

```python
import numpy as np
import jax, jax.numpy as jnp
from jax import lax

D_MODEL = 1024
BATCH = 8
SEQ = 4096
DEPTH = 4

N_MIXERS = 2
N_META = 16
GRID_W = 64
NA_HEADS = 16
NA_HEAD_DIM = D_MODEL // NA_HEADS
NA_WIN_H_MAX = 8
NA_WIN_W = 16
NA_KEY_BLOCK_W = 2 * NA_WIN_W
CONV_WIDTH = 31
D_FF = 4 * D_MODEL
N_CONV_LAYERS = (DEPTH + 1) // 2
N_NA_LAYERS = DEPTH // 2
EPS = 1e-6
NEG = -1e30

kernel_name = "hybrid_conv_natten_encoder"


def rms_norm(x, g):
    xf = x.astype(jnp.float32)
    y = xf * lax.rsqrt(jnp.mean(xf * xf, axis=-1, keepdims=True) + EPS)
    return (y * g.astype(jnp.float32)).astype(x.dtype)


def layer_norm(x, g, b):
    xf = x.astype(jnp.float32)
    mu = jnp.mean(xf, axis=-1, keepdims=True)
    xc = xf - mu
    y = xc * lax.rsqrt(jnp.mean(xc * xc, axis=-1, keepdims=True) + EPS)
    return (y * g.astype(jnp.float32) + b.astype(jnp.float32)).astype(x.dtype)


def conv_module(h, w_in, b_in, w_dw, b_dw, ln_g, ln_b, w_out, b_out):
    u = h @ w_in + b_in
    a, gate = jnp.split(u, 2, axis=-1)
    u = a * jax.nn.sigmoid(gate)
    u = lax.conv_general_dilated(
        u, w_dw[:, None, :].astype(u.dtype), window_strides=(1,),
        padding=[(CONV_WIDTH // 2, CONV_WIDTH // 2)],
        dimension_numbers=("NWC", "WIO", "NWC"),
        feature_group_count=D_MODEL) + b_dw
    u = jax.nn.silu(layer_norm(u, ln_g, ln_b))
    return u @ w_out + b_out


def _column_tables():
    n_cb = GRID_W // NA_WIN_W
    qc = np.arange(GRID_W).reshape(n_cb, NA_WIN_W)
    ks = np.clip(np.arange(n_cb) * NA_WIN_W - NA_WIN_W // 2, 0, GRID_W - NA_KEY_BLOCK_W)
    kc = ks[:, None] + np.arange(NA_KEY_BLOCK_W)
    sc = np.clip(qc - NA_WIN_W // 2, 0, GRID_W - NA_WIN_W)
    kcb = kc[:, None, :]
    valid = (kcb >= sc[..., None]) & (kcb < sc[..., None] + NA_WIN_W)
    dc = np.clip(kcb - qc[..., None], -(NA_WIN_W - 1), NA_WIN_W - 1) + NA_WIN_W - 1
    return [int(s) for s in ks], valid, dc


def neighbourhood_attention(h, w_qkv, w_o, rpb):
    B, L, _ = h.shape
    T = L - N_META
    rows = T // GRID_W
    kh = min(NA_WIN_H_MAX, rows)
    dt = h.dtype
    qkv = (h @ w_qkv).reshape(B, L, 3, NA_HEADS, NA_HEAD_DIM)
    q = jnp.moveaxis(qkv[:, :, 0], 1, 2) * (NA_HEAD_DIM ** -0.5)
    k = jnp.moveaxis(qkv[:, :, 1], 1, 2)
    v = jnp.moveaxis(qkv[:, :, 2], 1, 2)
    qm, km, vm = q[:, :, :N_META], k[:, :, :N_META], v[:, :, :N_META]
    grid = (B, NA_HEADS, rows, GRID_W, NA_HEAD_DIM)
    qg = q[:, :, N_META:].reshape(grid)
    kg = k[:, :, N_META:].reshape(grid)
    vg = v[:, :, N_META:].reshape(grid)

    s_mm = jnp.einsum("bhqd,bhkd->bhqk", qm, km).astype(jnp.float32)
    out_meta = jnp.einsum("bhqk,bhkd->bhqd", jax.nn.softmax(s_mm, axis=-1).astype(dt), vm)

    ks, valid, dc = _column_tables()
    n_cb = len(ks)
    mask_bias = jnp.where(jnp.asarray(valid), 0.0, NEG).astype(jnp.float32)
    mask_bias = mask_bias[None, None, :, :, None, :]

    def row_fn(r):
        sr = jnp.clip(r - kh // 2, 0, rows - kh)
        q_row = lax.dynamic_index_in_dim(qg, r, axis=2, keepdims=False)
        q_row = q_row.reshape(B, NA_HEADS, n_cb, NA_WIN_W, NA_HEAD_DIM)
        k_band = lax.dynamic_slice_in_dim(kg, sr, kh, axis=2)
        v_band = lax.dynamic_slice_in_dim(vg, sr, kh, axis=2)
        k_blk = jnp.stack([k_band[:, :, :, s:s + NA_KEY_BLOCK_W] for s in ks], axis=2)
        v_blk = jnp.stack([v_band[:, :, :, s:s + NA_KEY_BLOCK_W] for s in ks], axis=2)
        dr = sr + jnp.arange(kh) - r + NA_WIN_H_MAX - 1
        bias = jnp.take(rpb, dr, axis=1).astype(jnp.float32)[:, :, dc]
        bias = jnp.transpose(bias, (0, 2, 3, 1, 4))
        s_win = jnp.einsum("bhnqd,bhnikd->bhnqik", q_row, k_blk).astype(jnp.float32)
        s_win = (s_win + bias[None] + mask_bias).reshape(B, NA_HEADS, n_cb, NA_WIN_W, kh * NA_KEY_BLOCK_W)
        s_meta = jnp.einsum("bhnqd,bhmd->bhnqm", q_row, km).astype(jnp.float32)
        p = jax.nn.softmax(jnp.concatenate([s_meta, s_win], axis=-1), axis=-1)
        p_meta = p[..., :N_META].astype(dt)
        p_win = p[..., N_META:].reshape(B, NA_HEADS, n_cb, NA_WIN_W, kh, NA_KEY_BLOCK_W).astype(dt)
        o = (jnp.einsum("bhnqm,bhmd->bhnqd", p_meta, vm)
             + jnp.einsum("bhnqik,bhnikd->bhnqd", p_win, v_blk))
        return o.reshape(B, NA_HEADS, GRID_W, NA_HEAD_DIM)

    out_grid = lax.map(row_fn, jnp.arange(rows))
    out_grid = jnp.transpose(out_grid, (1, 2, 0, 3, 4)).reshape(B, NA_HEADS, T, NA_HEAD_DIM)
    o = jnp.concatenate([out_meta, out_grid], axis=2)
    o = jnp.moveaxis(o, 1, 2).reshape(B, L, NA_HEADS * NA_HEAD_DIM)
    return o @ w_o


def setup_inputs(seed: int = 0) -> dict:
    key = jax.random.key(seed)
    ks = jax.random.split(key, 20)
    D = D_MODEL
    nrm = jax.random.normal
    return {
        "x": nrm(ks[0], (BATCH, SEQ, D), jnp.float32),
        "meta_tokens": nrm(ks[1], (N_META, D), jnp.float32),
        "norm_mix_g": 1.0 + 0.02 * nrm(ks[2], (DEPTH, D), jnp.float32),
        "norm_mlp_g": 1.0 + 0.02 * nrm(ks[3], (DEPTH, D), jnp.float32),
        "conv_w_in": nrm(ks[4], (N_CONV_LAYERS, D, 2 * D), jnp.float32) * D ** -0.5,
        "conv_b_in": 0.01 * nrm(ks[5], (N_CONV_LAYERS, 2 * D), jnp.float32),
        "conv_w_dw": nrm(ks[6], (N_CONV_LAYERS, CONV_WIDTH, D), jnp.float32) * CONV_WIDTH ** -0.5,
        "conv_b_dw": 0.01 * nrm(ks[7], (N_CONV_LAYERS, D), jnp.float32),
        "conv_ln_g": 1.0 + 0.02 * nrm(ks[8], (N_CONV_LAYERS, D), jnp.float32),
        "conv_ln_b": 0.01 * nrm(ks[9], (N_CONV_LAYERS, D), jnp.float32),
        "conv_w_out": nrm(ks[10], (N_CONV_LAYERS, D, D), jnp.float32) * D ** -0.5,
        "conv_b_out": 0.01 * nrm(ks[11], (N_CONV_LAYERS, D), jnp.float32),
        "na_w_qkv": nrm(ks[12], (N_NA_LAYERS, D, 3 * D), jnp.float32) * D ** -0.5,
        "na_w_o": nrm(ks[13], (N_NA_LAYERS, D, D), jnp.float32) * D ** -0.5,
        "na_rpb": 0.02 * nrm(ks[14], (N_NA_LAYERS, NA_HEADS, 2 * NA_WIN_H_MAX - 1, 2 * NA_WIN_W - 1), jnp.float32),
        "mlp_w1": nrm(ks[15], (DEPTH, D, D_FF), jnp.float32) * D ** -0.5,
        "mlp_w2": nrm(ks[16], (DEPTH, D_FF, D), jnp.float32) * D_FF ** -0.5,
        "final_norm_g": 1.0 + 0.02 * nrm(ks[17], (D,), jnp.float32),
    }


def reference(x, meta_tokens, norm_mix_g, norm_mlp_g, conv_w_in, conv_b_in, conv_w_dw, conv_b_dw,
              conv_ln_g, conv_ln_b, conv_w_out, conv_b_out, na_w_qkv, na_w_o, na_rpb,
              mlp_w1, mlp_w2, final_norm_g):
    B = x.shape[0]
    meta = jnp.broadcast_to(meta_tokens.astype(x.dtype)[None], (B, N_META, D_MODEL))
    h = jnp.concatenate([meta, x], axis=1)
    for i in range(DEPTH):
        j = i // N_MIXERS
        hn = rms_norm(h, norm_mix_g[i])
        if i % N_MIXERS == 0:
            h = h + conv_module(hn, conv_w_in[j], conv_b_in[j], conv_w_dw[j], conv_b_dw[j],
                                conv_ln_g[j], conv_ln_b[j], conv_w_out[j], conv_b_out[j])
        else:
            h = h + neighbourhood_attention(hn, na_w_qkv[j], na_w_o[j], na_rpb[j])
        hn = rms_norm(h, norm_mlp_g[i])
        h = h + jnp.square(jax.nn.relu(hn @ mlp_w1[i])) @ mlp_w2[i]
    h = rms_norm(h, final_norm_g)
    return h[:, N_META:]
```

```python
import contextlib
import numpy as np
import concourse.bass as bass
import concourse.mybir as mybir
from concourse.bass_utils import run_bass_kernel_spmd

F32 = mybir.dt.float32
BF16 = mybir.dt.bfloat16
AF = mybir.ActivationFunctionType
ALU = mybir.AluOpType

D = 1024
NCH = 8
DFF = 4096
SEQ = 4096
NMETA = 16
PAD = 48
LT = 64 + SEQ
DEPTH = 4
EPS = 1e-6
CW = 31
NEG = -30000.0
TILES = [(PAD, NMETA)] + [(64 + 512 * i, 512) for i in range(8)]


class Sched:
    NDMA = 8

    def __init__(self, nc, es):
        self.nc = nc
        self.engs = {"pe": nc.tensor, "dve": nc.vector, "act": nc.scalar, "pool": nc.gpsimd, "sp": nc.sync}
        self.sem = {e: es.enter_context(nc.semaphore("s_" + e)) for e in self.engs}
        self.cnt = {e: 0 for e in self.engs}
        self.semobj = {("e", e): self.sem[e] for e in self.engs}
        self.dsem = {}
        self.dval = {}
        self.dnext = {}
        for q in ("sp", "pool"):
            self.dnext[q] = 0
            for i in range(self.NDMA):
                k = ("d", q, i)
                self.semobj[k] = es.enter_context(nc.semaphore("d_%s%d" % (q, i)))
                self.dval[k] = 0
        self.seen = {e: {} for e in self.engs}
        self.last_w = {}
        self.readers = {}
        self.nwaits = 0

    def _wait(self, e, sk, v):
        if self.seen[e].get(sk, 0) >= v:
            return
        self.engs[e].wait_ge(self.semobj[sk], v)
        self.seen[e][sk] = v
        self.nwaits += 1

    def _deps(self, e, reads, writes):
        own = ("e", e)
        deps = {}

        def add(sk, v):
            if deps.get(sk, 0) < v:
                deps[sk] = v
        for k in reads:
            w = self.last_w.get(k)
            if w is not None:
                add(*w)
        for k in writes:
            w = self.last_w.get(k)
            if w is not None and w[0] != own:
                add(*w)
            for sk, v in self.readers.get(k, {}).items():
                if sk != own:
                    add(sk, v)
        for sk, v in deps.items():
            self._wait(e, sk, v)

    def _record(self, tok, reads, writes):
        sk, v = tok
        for k in writes:
            self.last_w[k] = tok
            self.readers[k] = {}
        for k in reads:
            r = self.readers.setdefault(k, {})
            if r.get(sk, 0) < v:
                r[sk] = v

    def op(self, e, reads, writes, emit):
        self._deps(e, reads, writes)
        inst = emit(self.engs[e])
        self.cnt[e] += 1
        inst.then_inc(self.sem[e], 1)
        tok = (("e", e), self.cnt[e])
        self._record(tok, reads, writes)
        return tok

    def dma(self, q, out, in_, reads, writes):
        self._deps(q, reads, writes)
        i = self.dnext[q] % self.NDMA
        self.dnext[q] += 1
        k = ("d", q, i)
        self._wait(q, k, self.dval[k])
        self.engs[q].dma_start(out=out, in_=in_).then_inc(self.semobj[k], 16)
        self.dval[k] += 16
        tok = (k, self.dval[k])
        self._record(tok, reads, writes)
        return tok

    def barrier(self):
        for e in self.engs:
            for f in self.engs:
                if f != e and self.cnt[f] > 0:
                    self._wait(e, ("e", f), self.cnt[f])
            for k, v in self.dval.items():
                if v > 0:
                    self._wait(e, k, v)


def _pc(v):
    return np.ascontiguousarray(np.asarray(v, np.float32).reshape(NCH, 128).T)


class VecLayout:
    def __init__(self):
        self.cols = {}
        self.n = 0

    def add(self, name, ncols):
        self.cols[name] = (self.n, ncols)
        self.n += ncols


def vec_layout():
    L = VecLayout()
    for i in range(DEPTH):
        L.add(("mix_g", i), 8)
        L.add(("mlp_g", i), 8)
    L.add("final_g", 8)
    for j in range(2):
        L.add(("b_in", j), 16)
        L.add(("b_dw", j), 8)
        L.add(("ln_g", j), 8)
        L.add(("ln_b", j), 8)
        L.add(("b_out", j), 8)
        L.add(("w_dw", j), CW * 8)
    return L


class Builder:
    def __init__(self, layers=None, do_final=True):
        self.layers = list(range(DEPTH)) if layers is None else layers
        self.do_final = do_final
        self.VL = vec_layout()

    def build(self):
        nc = bass.Bass("TRN2", target_bir_lowering=False)
        self.nc = nc
        es = contextlib.ExitStack()
        self.es = es
        S = Sched(nc, es)
        self.S = S
        dt = nc.dram_tensor
        self.h0T = dt("h0T", [D, LT], F32, kind="ExternalInput").ap()
        self.vecs_d = dt("vecs", [128, self.VL.n], F32, kind="ExternalInput").ap()
        self.mlp_w1 = dt("mlp_w1", [DEPTH, D, DFF], F32, kind="ExternalInput").ap()
        self.mlp_w2 = dt("mlp_w2", [DEPTH, DFF, D], F32, kind="ExternalInput").ap()
        self.conv_w_in = dt("conv_w_in", [2, D, 2 * D], F32, kind="ExternalInput").ap()
        self.conv_w_out = dt("conv_w_out", [2, D, D], F32, kind="ExternalInput").ap()
        self.ident_d = dt("ident", [128, 128], F32, kind="ExternalInput").ap()
        self.wqkv_r = dt("wqkv_r", [2, 8, 128, 3072], F32, kind="ExternalInput").ap()
        self.na_w_o = dt("na_w_o", [2, D, D], F32, kind="ExternalInput").ap()
        self.na_bias = dt("na_bias", [2, 8, 128, 960], F32, kind="ExternalInput").ap()
        self.nac_d = dt("nac", [128, 65], F32, kind="ExternalInput").ap()
        self.outT = dt("outT", [D, SEQ], F32, kind="ExternalOutput").ap()

        sb = lambda name, shape, dtype: es.enter_context(nc.sbuf_tensor(name, shape, dtype))
        self.hT = sb("hT", [128, NCH, LT], F32)
        self.vecs = sb("vecs_sb", [128, self.VL.n], F32)
        self.ones = sb("ones", [128, 128], BF16)
        self.ident = sb("ident_sb", [128, 128], BF16)
        self.ps = [es.enter_context(nc.psum_tensor("ps%d" % i, [128, 512], F32)) for i in range(8)]

        S.op("dve", [], ["ones"], lambda e: e.memset(self.ones[:], 1.0))
        S.dma("sp", self.vecs[:], self.vecs_d, [], ["vecs"])
        S.dma("pool", self.ident[:], self.ident_d, [], ["ident"])
        h0v = self.h0T.rearrange("(c p) t -> p c t", p=128)
        for (a_, b_) in [(0, 576)] + [(576 + 512 * k, 576 + 512 * (k + 1)) for k in range(7)]:
            S.dma("sp", self.hT[:, :, a_:b_], h0v[:, :, a_:b_], [],
                  [k_ for c in range(NCH) for k_ in self.hkeys(c, a_, b_)])

        for i in self.layers:
            if i % 2 == 0:
                self.conv_layer(i)
            else:
                self.na_layer(i)
            self.mlp_layer(i)
        if self.do_final:
            self.final()
        S.barrier()
        es.close()
        return nc

    def vcol(self, name, c=0, n=1):
        o, _ = self.VL.cols[name]
        return self.vecs[:, o + c:o + c + n]

    def hkeys(self, c, a, b):
        return [("h", c, k) for k in range(a // 16, (b + 15) // 16)]

    def rmsnorm_cols(self, c0, n, gname, hn, hn_off, hnkey, W):
        S = self.S
        hT = self.hT
        ssb = W["ps_ss"]
        for c in range(NCH):
            j = W["sq_i"] % 2
            W["sq_i"] += 1
            sq = W["sq"][j]
            S.op("act", self.hkeys(c, c0, c0 + n), [("sq", j)],
                 lambda e, c=c, sq=sq: e.activation(out=sq[:, :n], in_=hT[:, c, c0:c0 + n], func=AF.Square))
            S.op("pe", [("sq", j), "ones"], [("ps", ssb)],
                 lambda e, c=c, sq=sq: e.matmul(self.ps[ssb][:, :n], lhsT=self.ones[:], rhs=sq[:, :n],
                                                start=(c == 0), stop=(c == NCH - 1)))
        rt = W["rt"]
        S.op("act", [("ps", ssb), "epsc"], ["rt"],
             lambda e: e.activation(out=rt[:, :n], in_=self.ps[ssb][:, :n], func=AF.Sqrt,
                                    scale=1.0 / D, bias=W["eps"][:, 0:1]))
        S.op("dve", ["rt"], ["rstd"], lambda e: e.reciprocal(out=W["rstd"][:, :n], in_=rt[:, :n]))
        for c in range(NCH):
            S.op("dve", self.hkeys(c, c0, c0 + n) + ["rstd", "vecs"], [hnkey],
                 lambda e, c=c: e.scalar_tensor_tensor(
                     out=hn[:, c, hn_off:hn_off + n], in0=hT[:, c, c0:c0 + n],
                     scalar=self.vcol(gname, c), in1=W["rstd"][:, :n], op0=ALU.mult, op1=ALU.mult))

    def rmsnorm_tile(self, t, gname, hn, hn_off, hnkey, W):
        c0, n = TILES[t]
        self.rmsnorm_cols(c0, n, gname, hn, hn_off, hnkey, W)

    def norm_work(self, sbt, nmax=512):
        W = {"sq": [sbt("sq%d" % j, [128, nmax], BF16) for j in range(2)], "sq_i": 0,
             "rt": sbt("rt", [128, nmax], F32), "rstd": sbt("rstd", [128, nmax], F32),
             "eps": sbt("epsc", [128, 1], F32), "ps_ss": 7}
        self.S.op("dve", [], ["epsc"], lambda e: e.memset(W["eps"][:], EPS))
        return W

    def mlp_layer(self, i):
        S = self.S
        nc = self.nc
        S.barrier()
        with contextlib.ExitStack() as ls:
            sbt = lambda name, shape, dtype: ls.enter_context(nc.sbuf_tensor("m%d_%s" % (i, name), shape, dtype))
            W = self.norm_work(sbt)
            NSLOT = 3
            hn = sbt("hn", [128, NCH, NSLOT * 512], BF16)
            w1 = [sbt("w1_%d" % j, [128, NCH, 512], BF16) for j in range(2)]
            w2 = [sbt("w2_%d" % j, [128, 4, D], BF16) for j in range(2)]
            ut = [sbt("ut%d" % j, [128, 4, 512], BF16) for j in range(2)]
            rl = [sbt("rl%d" % j, [128, 512], F32) for j in range(2)]
            blocks = [[0, 1, 2], [3, 4, 5], [6, 7, 8]]
            w1d = self.mlp_w1[i].rearrange("(kc p) f -> p kc f", p=128)
            w2d = self.mlp_w2[i].rearrange("(fc p) o -> p fc o", p=128)
            gi = 0

            def load_w(g, j):
                S.dma("pool", w1[j][:], w1d[:, :, g * 512:(g + 1) * 512], [], [("w1", j)])
                S.dma("pool", w2[j][:], w2d[:, g * 4:(g + 1) * 4, :], [], [("w2", j)])

            seq = [(b, g) for b in range(len(blocks)) for g in range(8)]
            load_w(seq[0][1], 0)
            hid_i = 0
            out_i = 0
            ut_i = 0
            for si, (b, g) in enumerate(seq):
                blk = blocks[b]
                if g == 0:
                    for s, t in enumerate(blk):
                        self.rmsnorm_tile(t, ("mlp_g", i), hn, s * 512, ("hn", s), W)
                j = si % 2
                if si + 1 < len(seq):
                    load_w(seq[si + 1][1], (si + 1) % 2)

                def hid(s, t, uj):
                    nonlocal hid_i
                    c0, n = TILES[t]
                    for fc in range(4):
                        pb = hid_i % 2
                        hid_i += 1
                        def mm(e, fc=fc, pb=pb):
                            last = None
                            for kc in range(NCH):
                                last = e.matmul(self.ps[pb][:, :n], lhsT=w1[j][:, kc, fc * 128:(fc + 1) * 128],
                                                rhs=hn[:, kc, s * 512:s * 512 + n], start=(kc == 0), stop=(kc == NCH - 1))
                            return last
                        S.op("pe", [("w1", j), ("hn", s)], [("ps", pb)], mm)
                        S.op("act", [("ps", pb)], [("rl", pb)],
                             lambda e, pb=pb: e.activation(out=rl[pb][:, :n], in_=self.ps[pb][:, :n], func=AF.Relu))
                        S.op("act", [("rl", pb)], [("ut", uj, fc)],
                             lambda e, pb=pb, fc=fc: e.activation(out=ut[uj][:, fc, :n], in_=rl[pb][:, :n], func=AF.Square))

                def outp(s, t, uj):
                    nonlocal out_i
                    c0, n = TILES[t]
                    for oc in range(NCH):
                        pb = 2 + out_i % 4
                        out_i += 1
                        def mm(e, oc=oc, pb=pb):
                            last = None
                            for fc in range(4):
                                last = e.matmul(self.ps[pb][:, :n], lhsT=w2[j][:, fc, oc * 128:(oc + 1) * 128],
                                                rhs=ut[uj][:, fc, :n], start=(fc == 0), stop=(fc == 3))
                            return last
                        S.op("pe", [("w2", j)] + [("ut", uj, fc) for fc in range(4)], [("ps", pb)], mm)
                        S.op("dve", [("ps", pb)] + self.hkeys(oc, c0, c0 + n), self.hkeys(oc, c0, c0 + n),
                             lambda e, oc=oc, pb=pb: e.tensor_tensor(
                                 out=self.hT[:, oc, c0:c0 + n], in0=self.ps[pb][:, :n],
                                 in1=self.hT[:, oc, c0:c0 + n], op=ALU.add))

                pend = None
                for s, t in enumerate(blk):
                    uj = ut_i % 2
                    ut_i += 1
                    hid(s, t, uj)
                    if pend is not None:
                        outp(*pend)
                    pend = (s, t, uj)
                outp(*pend)

    def rmsnorm_gen(self, c0, n, gname, hn, hn_off, hnkey, W):
        S = self.S
        hT = self.hT
        ssb = W["ps_ss"]
        for c in range(NCH):
            j = W["sq_i"] % 2
            W["sq_i"] += 1
            sq = W["sq"][j]
            S.op("act", self.hkeys(c, c0, c0 + n), [("sq", j)],
                 lambda e, c=c, sq=sq: e.activation(out=sq[:, :n], in_=hT[:, c, c0:c0 + n], func=AF.Square))
            S.op("pe", [("sq", j), "ones"], [("ps", ssb)],
                 lambda e, c=c, sq=sq: e.matmul(self.ps[ssb][:, :n], lhsT=self.ones[:], rhs=sq[:, :n],
                                                start=(c == 0), stop=(c == NCH - 1)))
            yield
        rt = W["rt"]
        S.op("act", [("ps", ssb), "epsc"], ["rt"],
             lambda e: e.activation(out=rt[:, :n], in_=self.ps[ssb][:, :n], func=AF.Sqrt,
                                    scale=1.0 / D, bias=W["eps"][:, 0:1]))
        S.op("dve", ["rt"], ["rstd"], lambda e: e.reciprocal(out=W["rstd"][:, :n], in_=rt[:, :n]))
        yield
        for c in range(NCH):
            S.op("dve", self.hkeys(c, c0, c0 + n) + ["rstd", "vecs"], [hnkey],
                 lambda e, c=c: e.scalar_tensor_tensor(
                     out=hn[:, c, hn_off:hn_off + n], in0=hT[:, c, c0:c0 + n],
                     scalar=self.vcol(gname, c), in1=W["rstd"][:, :n], op0=ALU.mult, op1=ALU.mult))
            yield

    @staticmethod
    def interleave(*gens, ratio=None):
        gens = [g for g in gens if g is not None]
        alive = list(gens)
        while alive:
            for g in list(alive):
                try:
                    next(g)
                except StopIteration:
                    alive.remove(g)

    def conv_layer(self, i):
        S = self.S
        nc = self.nc
        jl = i // 2
        S.barrier()
        N = 256
        with contextlib.ExitStack() as ls:
            sbt = lambda name, shape, dtype: ls.enter_context(nc.sbuf_tensor("c%d_%s" % (i, name), shape, dtype))
            W = self.norm_work(sbt, N)
            W["ps_ss"] = 3
            Gb = sbt("Gb", [128, NCH, 30 + N], BF16)
            ctmp = sbt("ctmp", [128, NCH, 30], BF16)
            hn = sbt("hn", [128, NCH, N], BF16)
            sT = sbt("sT", [128, NCH, N], BF16)
            y = sbt("y", [128, NCH, N], F32)
            ybf = [sbt("ybf%d" % k, [128, N], BF16) for k in range(2)]
            ysq = [sbt("ysq%d" % k, [128, N], BF16) for k in range(2)]
            dg = [sbt("dg%d" % k, [128, 8, 128], BF16) for k in range(4)]
            mu = sbt("mu", [128, N], F32)
            vb = sbt("vb", [128, N], F32)
            rs = sbt("rs", [128, N], F32)
            nm = sbt("nm", [128, N], F32)
            sg = [sbt("sg%d" % k, [128, N], F32) for k in range(2)]
            sgz = [sbt("sgz%d" % k, [128, N], F32) for k in range(2)]
            NWB = 4
            wb = [sbt("wb%d" % k, [128, NCH, 512], BF16) for k in range(NWB)]
            win = self.conv_w_in[jl].rearrange("(kc p) f -> p kc f", p=128)
            wout = self.conv_w_out[jl].rearrange("(kc p) f -> p kc f", p=128)
            steps = [(PAD + 242 * k, 242) for k in range(16)] + [(PAD + 242 * 16, 240)]
            assert steps[-1][0] + steps[-1][1] == LT
            nreal = len(steps)
            LAST = len(steps) - 1
            inl = [win[:, :, 0:512], win[:, :, 1024:1536], win[:, :, 512:1024], win[:, :, 1536:2048]]
            outl = [wout[:, :, 0:512], wout[:, :, 512:1024]]
            loads = list(inl)
            for st in range(len(steps)):
                if st >= 1:
                    loads += outl
                if st + 1 < nreal:
                    loads += inl
            loads += outl
            issued = [0]
            consumed = [0]

            def ensure(idx):
                while issued[0] <= min(idx, len(loads) - 1):
                    k = issued[0]
                    S.dma("pool", wb[k % NWB][:], loads[k], [], [("wb", k % NWB)])
                    issued[0] += 1
            lptr = [0]

            def next_w():
                k = lptr[0]
                lptr[0] += 1
                ensure(k)
                return k % NWB

            def done_w(nloads):
                consumed[0] += nloads
                ensure(consumed[0] + NWB - 1)

            S.op("dve", [], [("G", c) for c in range(NCH)], lambda e: e.memset(Gb[:, :, 0:30], 0.0))
            cs = {"inb": 0, "obk": 0, "yi": 0}

            def geom(st):
                c0, n = steps[st]
                m0 = 15 if st == 0 else 0
                nn = n - m0 + (15 if st == LAST else 0)
                a = c0 - 15 + m0
                return c0, n, m0, nn, a, a + nn

            def rms_gen(st):
                if st < nreal:
                    c0, n = steps[st]
                    yield from self.rmsnorm_gen(c0, n, ("mix_g", i), hn, 0, "hnc", W)

            def inproj_gen(st):
                c0, n, m0, nn, a, b = geom(st)
                if st == LAST:
                    S.op("dve", [], [("G", c) for c in range(NCH)],
                         lambda e: e.memset(Gb[:, :, 30 + n:30 + n + 16], 0.0))
                for hf in range(2):
                    ba = next_w()
                    bg = next_w()
                    for q in range(4):
                        oc = hf * 4 + q
                        pa = cs["inb"] % 4
                        cs["inb"] += 1
                        def mmag(e, q=q, pa=pa, ba=ba, bg=bg):
                            last = None
                            for kc in range(NCH):
                                last = e.matmul(self.ps[pa][:, 0:n], lhsT=wb[ba][:, kc, q * 128:(q + 1) * 128],
                                                rhs=hn[:, kc, :n], start=(kc == 0), stop=(kc == NCH - 1))
                            for kc in range(NCH):
                                last = e.matmul(self.ps[pa][:, 256:256 + n], lhsT=wb[bg][:, kc, q * 128:(q + 1) * 128],
                                                rhs=hn[:, kc, :n], start=(kc == 0), stop=(kc == NCH - 1))
                            return last
                        S.op("pe", [("wb", ba), ("wb", bg), "hnc"], [("ps", pa)], mmag)
                        sgi = cs["inb"] % 2
                        S.op("act", [("ps", pa), "vecs"], [("sg", sgi)],
                             lambda e, pa=pa, oc=oc, sgi=sgi: e.activation(
                                 out=sg[sgi][:, :n], in_=self.ps[pa][:, 256:256 + n], func=AF.Sigmoid,
                                 bias=self.vcol(("b_in", jl), 8 + oc)))
                        S.op("dve", [("ps", pa), ("sg", sgi), "vecs"], [("G", oc)],
                             lambda e, pa=pa, oc=oc, sgi=sgi: e.scalar_tensor_tensor(
                                 out=Gb[:, oc, 30:30 + n], in0=self.ps[pa][:, 0:n],
                                 scalar=self.vcol(("b_in", jl), oc), in1=sg[sgi][:, :n],
                                 op0=ALU.add, op1=ALU.mult))
                        yield
                    done_w(2)

            pending = [None]

            def builds(c):
                for qt in range(4):
                    for k in range(8 * qt, min(8 * qt + 8, CW)):
                        wcol = self.vcol(("w_dw", jl), k * 8 + c)
                        dkey = ("dg", qt, k % 8)
                        if k % 8 in (3, 7):
                            S.op("act", ["ident", "vecs"], [dkey],
                                 lambda e, k=k, qt=qt, wcol=wcol: e.activation(
                                     out=dg[qt][:, k % 8, :], in_=self.ident[:], func=AF.Copy, scale=wcol))
                        elif k % 8 in (1, 5):
                            S.op("pool", ["ident", "vecs"], [dkey],
                                 lambda e, k=k, qt=qt, wcol=wcol: e.tensor_scalar(
                                     out=dg[qt][:, k % 8, :], in0=self.ident[:], scalar1=wcol, scalar2=0.0,
                                     op0=ALU.mult, op1=ALU.add))
                        else:
                            S.op("dve", ["ident", "vecs"], [dkey],
                                 lambda e, k=k, qt=qt, wcol=wcol: e.tensor_scalar(
                                     out=dg[qt][:, k % 8, :], in0=self.ident[:], scalar1=wcol, scalar2=None,
                                     op0=ALU.mult))

            def conv_gen(st):
                c0, n, m0, nn, a, b = geom(st)
                for c in range(NCH):
                    for qt in range(4):
                        ks = list(range(8 * qt, min(8 * qt + 8, CW)))
                        pb = 4 + c % 2
                        def mmc(e, c=c, ks=ks, qt=qt, pb=pb):
                            last = None
                            for k in ks:
                                last = e.matmul(self.ps[pb][:, :nn], lhsT=dg[qt][:, k % 8, :],
                                                rhs=Gb[:, c, m0 + k:m0 + k + nn], start=(k == 0), stop=(k == CW - 1))
                            return last
                        S.op("pe", [("dg", qt, k % 8) for k in ks] + [("G", c)], [("ps", pb)], mmc)
                    if not (st == len(steps) - 1 and c == NCH - 1):
                        builds((c + 1) % NCH)
                    if pending[0] is not None:
                        pending[0]()
                        pending[0] = None
                    pb = 4 + c % 2
                    bdw = self.vcol(("b_dw", jl), c)
                    yj = cs["yi"] % 2
                    cs["yi"] += 1
                    S.op("act", [("ps", pb), "vecs"], [("y", c)],
                         lambda e, c=c, pb=pb, bdw=bdw: e.activation(out=y[:, c, :nn], in_=self.ps[pb][:, :nn],
                                                                     func=AF.Identity, bias=bdw))
                    S.op("act", [("ps", pb), "vecs"], [("ybf", yj)],
                         lambda e, pb=pb, bdw=bdw, yj=yj: e.activation(out=ybf[yj][:, :nn], in_=self.ps[pb][:, :nn],
                                                                       func=AF.Identity, bias=bdw))
                    S.op("act", [("ps", pb), "vecs"], [("ysq", yj)],
                         lambda e, pb=pb, bdw=bdw, yj=yj: e.activation(out=ysq[yj][:, :nn], in_=self.ps[pb][:, :nn],
                                                                       func=AF.Square, bias=bdw))
                    def stats(c=c, yj=yj):
                        S.op("pe", [("ybf", yj), "ones"], [("ps", 6)],
                             lambda e: e.matmul(self.ps[6][:, :nn], lhsT=self.ones[:], rhs=ybf[yj][:, :nn],
                                                start=(c == 0), stop=(c == NCH - 1)))
                        S.op("pe", [("ysq", yj), "ones"], [("ps", 7)],
                             lambda e: e.matmul(self.ps[7][:, :nn], lhsT=self.ones[:], rhs=ysq[yj][:, :nn],
                                                start=(c == 0), stop=(c == NCH - 1)))
                    pending[0] = stats
                    yield
                pending[0]()
                pending[0] = None
                if st != LAST:
                    S.op("dve", [("G", c) for c in range(NCH)], ["ctmp"],
                         lambda e: e.tensor_copy(out=ctmp[:], in_=Gb[:, :, n:n + 30]))
                    S.op("dve", ["ctmp"], [("G", c) for c in range(NCH)],
                         lambda e: e.tensor_copy(out=Gb[:, :, 0:30], in_=ctmp[:]))

            def ln_gen(st):
                c0, n, m0, nn, a, b = geom(st)
                S.op("act", [("ps", 6)], ["mu"],
                     lambda e: e.activation(out=mu[:, :nn], in_=self.ps[6][:, :nn], func=AF.Copy, scale=1.0 / D))
                S.op("dve", ["mu"], ["vb"],
                     lambda e: e.tensor_tensor(out=vb[:, :nn], in0=mu[:, :nn], in1=mu[:, :nn], op=ALU.mult))
                S.op("dve", [("ps", 7), "vb"], ["vb"],
                     lambda e: e.scalar_tensor_tensor(out=vb[:, :nn], in0=self.ps[7][:, :nn], scalar=1.0 / D,
                                                      in1=vb[:, :nn], op0=ALU.mult, op1=ALU.subtract))
                yield
                S.op("act", ["vb", "epsc"], ["vb"],
                     lambda e: e.activation(out=vb[:, :nn], in_=vb[:, :nn], func=AF.Sqrt, bias=W["eps"][:, 0:1]))
                S.op("dve", ["vb"], ["rs"], lambda e: e.reciprocal(out=rs[:, :nn], in_=vb[:, :nn]))
                S.op("dve", ["mu", "rs"], ["nm"],
                     lambda e: e.scalar_tensor_tensor(out=nm[:, :nn], in0=mu[:, :nn], scalar=-1.0, in1=rs[:, :nn],
                                                      op0=ALU.mult, op1=ALU.mult))
                yield

            def ln_chunks_gen(st):
                c0, n, m0, nn, a, b = geom(st)
                for c in range(NCH):
                    S.op("dve", [("y", c), "rs"], [("y", c)],
                         lambda e, c=c: e.tensor_tensor(out=y[:, c, :nn], in0=y[:, c, :nn], in1=rs[:, :nn], op=ALU.mult))
                    S.op("dve", [("y", c), "nm"], [("y", c)],
                         lambda e, c=c: e.tensor_tensor(out=y[:, c, :nn], in0=y[:, c, :nn], in1=nm[:, :nn], op=ALU.add))
                    zi = c % 2
                    S.op("act", [("y", c), "vecs"], [("sgz", zi)],
                         lambda e, c=c, zi=zi: e.activation(out=sgz[zi][:, :nn], in_=y[:, c, :nn], func=AF.Sigmoid,
                                                            scale=self.vcol(("ln_g", jl), c),
                                                            bias=self.vcol(("ln_b", jl), c)))
                    S.op("act", [("y", c), "vecs"], [("y", c)],
                         lambda e, c=c: e.activation(out=y[:, c, :nn], in_=y[:, c, :nn], func=AF.Identity,
                                                     scale=self.vcol(("ln_g", jl), c), bias=self.vcol(("ln_b", jl), c)))
                    S.op("pool", [("y", c), ("sgz", zi)], [("sT", c)],
                         lambda e, c=c, zi=zi: e.tensor_tensor(out=sT[:, c, :nn], in0=y[:, c, :nn],
                                                               in1=sgz[zi][:, :nn], op=ALU.mult))
                    yield

            def outproj_gen(st):
                c0, n, m0, nn, a, b = geom(st)
                for hf in range(2):
                    bo = next_w()
                    for q in range(4):
                        oc = hf * 4 + q
                        pb = cs["obk"] % 4
                        cs["obk"] += 1
                        def mmo(e, q=q, pb=pb, bo=bo):
                            last = None
                            for kc in range(NCH):
                                last = e.matmul(self.ps[pb][:, :nn], lhsT=wb[bo][:, kc, q * 128:(q + 1) * 128],
                                                rhs=sT[:, kc, :nn], start=(kc == 0), stop=(kc == NCH - 1))
                            return last
                        S.op("pe", [("wb", bo)] + [("sT", c) for c in range(NCH)], [("ps", pb)], mmo)
                        hk = self.hkeys(oc, a, b)
                        S.op("dve", [("ps", pb), "vecs"] + hk, hk,
                             lambda e, oc=oc, pb=pb: e.scalar_tensor_tensor(
                                 out=self.hT[:, oc, a:b], in0=self.ps[pb][:, :nn], scalar=self.vcol(("b_out", jl), oc),
                                 in1=self.hT[:, oc, a:b], op0=ALU.add, op1=ALU.add))
                        yield
                    done_w(1)

            def seq(*gens):
                for g in gens:
                    if g is not None:
                        yield from g

            ensure(NWB - 1)
            self.interleave(rms_gen(0))
            builds(0)
            self.interleave(inproj_gen(0))
            for st in range(len(steps)):
                nxt = st + 1 < len(steps)
                cg = conv_gen(st)
                rg = rms_gen(st + 1) if nxt else iter(())
                lc = ln_chunks_gen(st - 1) if st >= 1 else iter(())
                for _c in range(4):
                    next(lc, None)
                    next(lc, None)
                    next(cg)
                    for _k in range(5):
                        next(rg, None)
                for _ in rg:
                    pass
                if st >= 1:
                    for _ in outproj_gen(st - 1):
                        pass
                for _ in cg:
                    pass
                for _ in ln_gen(st):
                    pass
                if nxt:
                    for _ in inproj_gen(st + 1):
                        pass
            for _ in ln_chunks_gen(len(steps) - 1):
                pass
            for _ in outproj_gen(len(steps) - 1):
                pass

    def na_layer(self, i):
        S = self.S
        nc = self.nc
        jl = i // 2
        S.barrier()
        with contextlib.ExitStack() as ls:
            sbt = lambda name, shape, dtype: ls.enter_context(nc.sbuf_tensor("a%d_%s" % (i, name), shape, dtype))
            W = self.norm_work(sbt, 512)
            T = sbt("T", [128, 8, 960], BF16)
            nac = sbt("nac", [128, 65], F32)
            zer = sbt("zer", [128, 128], BF16)
            obd = sbt("obd", [128, 128], BF16)
            S.dma("sp", nac[:], self.nac_d, [], ["nac"])
            S.op("dve", [], ["zer"], lambda e: e.memset(zer[:], 0.0))
            S.op("dve", [], ["obd"], lambda e: e.memset(obd[:], 0.0))
            S.op("dve", ["obd"], ["obd"], lambda e: e.memset(obd[0:64, 0:64], 1.0))
            S.op("dve", ["obd"], ["obd"], lambda e: e.memset(obd[64:128, 64:128], 1.0))
            with contextlib.ExitStack() as ls2:
                stg = [ls2.enter_context(nc.sbuf_tensor("a%d_stg%d" % (i, k), [128, 960], F32)) for k in range(2)]
                for pr in range(8):
                    S.dma("sp", stg[pr % 2][:], self.na_bias[jl, pr], [], [("stg", pr % 2)])
                    for ii in range(15):
                        S.op("dve", [("stg", pr % 2), "nac"], [("T", pr)],
                             lambda e, pr=pr, ii=ii: e.tensor_tensor(
                                 out=T[:, pr, ii * 64:(ii + 1) * 64], in0=stg[pr % 2][:, ii * 64:(ii + 1) * 64],
                                 in1=nac[:, 0:64], op=ALU.add))
                S.barrier()
            hnr = sbt("hnr", [128, NCH, 3 * 512], BF16)
            hnm = sbt("hnm", [128, NCH, 64], BF16)
            wq = [sbt("wq%d" % k, [128, NCH, 384], BF16) for k in range(2)]
            wo = sbt("wo", [128, D], BF16)
            qT = sbt("qT", [128, 512], BF16)
            Kbd = sbt("Kbd", [128, 16, 128], BF16)
            Vbd = sbt("Vbd", [128, 16, 128], BF16)
            PT = [sbt("PT%d" % k, [128, 512], BF16) for k in range(2)]
            OT = sbt("OT", [128, 512], BF16)
            S.op("dve", [], ["Kbd"], lambda e: e.memset(Kbd[:], 0.0))
            S.op("dve", [], [("Vbd", 0, 0), ("Vbd", 0, 1), ("Vbd", 8, 0), ("Vbd", 8, 1)],
                 lambda e: e.memset(Vbd[:], 0.0))
            ps = self.ps

            def dup(ap):
                return bass.AP(ap.tensor, ap.offset, (tuple(ap.ap[0]), (0, 2), (1, 64)))

            def sr(rq):
                return min(max(rq - 4, 0), 56)

            self.rmsnorm_cols(0, 64, ("mix_g", i), hnm, 0, "hnm", W)
            self.rmsnorm_tile(1, ("mix_g", i), hnr, 512, ("hnr", 1), W)
            self.rmsnorm_tile(2, ("mix_g", i), hnr, 1024, ("hnr", 2), W)

            def rowsrc(rk):
                tk = rk // 8 + 1
                return ("hnr", tk % 3), (tk % 3) * 512 + (rk % 8) * 64

            items = [(-1, p) for p in range(8)] + [(j, p) for j in range(8) for p in range(8)]
            wqd = self.wqkv_r[jl]
            lnd = W["rt"]

            def load_wq(idx):
                S.dma("pool", wq[idx % 2][:], wqd[items[idx][1]].rearrange("p (kc f) -> p kc f", kc=NCH),
                      [], [("wq", idx % 2)])

            def load_wo(idx):
                p = items[idx][1]
                S.dma("pool", wo[:], self.na_w_o[jl][p * 128:(p + 1) * 128, :], [], ["wo"])
            cnt = {"pt": 0, "sb": 0, "ob": 0, "kb": 0}
            info = {}

            def proj(idx):
                j, p = items[idx]
                wb_ = wq[idx % 2]
                wk = ("wq", idx % 2)
                if j < 0:
                    N = 64
                    qkey, qoff, qsrc = "hnm", 0, hnm
                    rows = []
                    rqs = []
                else:
                    N = 512
                    tq = j + 1
                    if p == 0 and tq + 1 <= 8 and tq + 1 > 2:
                        self.rmsnorm_tile(tq + 1, ("mix_g", i), hnr, ((tq + 1) % 3) * 512, ("hnr", (tq + 1) % 3), W)
                    qkey, qoff, qsrc = ("hnr", tq % 3), (tq % 3) * 512, hnr
                    rqs = list(range(8 * j, 8 * j + 8))
                    lo = min(sr(r) for r in rqs)
                    hi = max(sr(r) + 7 for r in rqs)
                    rows = list(range(lo, hi + 1))
                nslots = 1 + len(rows)
                info[idx] = (N, rows, rqs, nslots)
                def mmq(e):
                    last = None
                    for kc in range(NCH):
                        last = e.matmul(ps[0][:, :N], lhsT=wb_[:, kc, 0:128], rhs=qsrc[:, kc, qoff:qoff + N],
                                        start=(kc == 0), stop=(kc == NCH - 1))
                    return last
                S.op("pe", [wk, qkey], [("ps", 0)], mmq)
                S.op("act", [("ps", 0)], ["qT"],
                     lambda e: e.activation(out=qT[:, :N], in_=ps[0][:, :N], func=AF.Copy, scale=0.125))
                segs = [("hnm", hnm, 0, 0, 1)]
                r = 0
                while r < len(rows):
                    rk = rows[r]
                    tk = rk // 8 + 1
                    nr = 1
                    while r + nr < len(rows) and (rows[r + nr] // 8 + 1) == tk:
                        nr += 1
                    key, off = rowsrc(rk)
                    segs.append((key, hnr, off, 1 + r, nr))
                    r += nr
                for (key, src, off, s0, nr) in segs:
                    pb = 1 + cnt["kb"] % 2
                    cnt["kb"] += 1
                    ncol = nr * 64
                    def mmk(e, src=src, off=off, ncol=ncol, pb=pb):
                        last = None
                        for kc in range(NCH):
                            last = e.matmul(ps[pb][:, :ncol], lhsT=wb_[:, kc, 128:256], rhs=src[:, kc, off:off + ncol],
                                            start=(kc == 0), stop=(kc == NCH - 1))
                        return last
                    S.op("pe", [wk, key], [("ps", pb)], mmk)
                    S.op("dve", [("ps", pb)], ["Kbd"],
                         lambda e, pb=pb, s0=s0, nr=nr, ncol=ncol: e.tensor_copy(
                             out=Kbd[0:64, s0:s0 + nr, 0:64],
                             in_=ps[pb][0:64, 0:ncol].rearrange("p (r c) -> p r c", c=64)))
                    S.op("act", [("ps", pb)], ["Kbd"],
                         lambda e, pb=pb, s0=s0, nr=nr, ncol=ncol: e.activation(
                             out=Kbd[64:128, s0:s0 + nr, 64:128],
                             in_=ps[pb][64:128, 0:ncol].rearrange("p (r c) -> p r c", c=64), func=AF.Copy))
                vsrc = [("hnm", hnm, 0)] + [(rowsrc(rk)[0], hnr, rowsrc(rk)[1]) for rk in rows]
                for g0 in range(0, nslots, 8):
                    g = min(8, nslots - g0)
                    vb_ = 3 if g0 == 0 else 0
                    keys = list({vsrc[g0 + q][0] for q in range(g)})
                    def mmv(e, g0=g0, g=g, vb_=vb_):
                        last = None
                        for q in range(g):
                            _, src, off = vsrc[g0 + q]
                            for kc in range(NCH):
                                e.matmul(ps[vb_][0:64, q * 64:(q + 1) * 64], lhsT=src[:, kc, off:off + 64],
                                         rhs=wb_[:, kc, 256:320], start=(kc == 0), stop=(kc == NCH - 1))
                                last = e.matmul(ps[vb_][64:128, q * 64:(q + 1) * 64], lhsT=src[:, kc, off:off + 64],
                                                rhs=wb_[:, kc, 320:384], start=(kc == 0), stop=(kc == NCH - 1))
                        return last
                    S.op("pe", [wk] + keys, [("ps", vb_)], mmv)
                    S.op("dve", [("ps", vb_)], [("Vbd", g0, 0)],
                         lambda e, g0=g0, g=g, vb_=vb_: e.tensor_copy(
                             out=Vbd[0:64, g0:g0 + g, 0:64],
                             in_=ps[vb_][0:64, 0:g * 64].rearrange("p (q f) -> p q f", f=64)))
                    S.op("act", [("ps", vb_)], [("Vbd", g0, 1)],
                         lambda e, g0=g0, g=g, vb_=vb_: e.activation(
                             out=Vbd[64:128, g0:g0 + g, 64:128],
                             in_=ps[vb_][64:128, 0:g * 64].rearrange("p (q f) -> p q f", f=64), func=AF.Copy))

            def attention(idx):
                j, p = items[idx]
                N, rows, rqs, nslots = info[idx]
                def zinit(e):
                    e.matmul(ps[6][:, :N], lhsT=zer[:], rhs=qT[:, :N], start=True, stop=False)
                    return e.matmul(ps[7][:, :N], lhsT=zer[:], rhs=qT[:, :N], start=True, stop=False)
                S.op("pe", ["zer", "qT"], [("ps", 6), ("ps", 7)], zinit)
                geo = []
                for s_ in range(nslots):
                    if s_ == 0:
                        geo.append((0, N, None))
                    else:
                        rk = rows[s_ - 1]
                        att = [rq for rq in rqs if sr(rq) <= rk <= sr(rq) + 7]
                        qa, qb = att[0], att[-1]
                        assert att == list(range(qa, qb + 1))
                        i0 = 7 - rk + qa
                        assert 0 <= i0 and i0 + (qb - qa) <= 14
                        geo.append(((qa - 8 * j) * 64, (qb - qa + 1) * 64, i0))
                sbank = {}

                SB = [4, 5, 1]

                def smm(s_):
                    cq0, ncq, i0 = geo[s_]
                    pb = SB[cnt["sb"] % 3]
                    cnt["sb"] += 1
                    sbank[s_] = pb
                    S.op("pe", ["Kbd", "qT"], [("ps", pb)],
                         lambda e: e.matmul(ps[pb][:, :ncq], lhsT=Kbd[:, s_, :], rhs=qT[:, cq0:cq0 + ncq],
                                            start=True, stop=True))
                    if i0 is not None:
                        S.op("dve", [("ps", pb), ("T", p)], [("ps", pb)],
                             lambda e: e.tensor_tensor(out=ps[pb][:, :ncq], in0=ps[pb][:, :ncq],
                                                       in1=T[:, p, i0 * 64:i0 * 64 + ncq], op=ALU.add))
                smm(0)
                if nslots > 1:
                    smm(1)
                for s_ in range(nslots):
                    if s_ + 2 < nslots:
                        smm(s_ + 2)
                    cq0, ncq, i0 = geo[s_]
                    pb = sbank[s_]
                    pj = cnt["pt"] % 2
                    cnt["pt"] += 1
                    if s_ == 0:
                        S.op("act", [("ps", pb), "nac"], [("PT", pj)],
                             lambda e, pb=pb, pj=pj, ncq=ncq: e.activation(out=PT[pj][:, :ncq], in_=ps[pb][:, :ncq],
                                                                           func=AF.Exp, bias=nac[:, 64:65]))
                    else:
                        S.op("act", [("ps", pb)], [("PT", pj)],
                             lambda e, pb=pb, pj=pj, ncq=ncq: e.activation(out=PT[pj][:, :ncq], in_=ps[pb][:, :ncq],
                                                                           func=AF.Exp))
                    lastslot = s_ == nslots - 1
                    def mmpv(e, s_=s_, cq0=cq0, ncq=ncq, pj=pj, lastslot=lastslot):
                        e.matmul(ps[6][:, cq0:cq0 + ncq], lhsT=Vbd[:, s_, :], rhs=PT[pj][:, :ncq],
                                 start=False, stop=lastslot)
                        return e.matmul(ps[7][:, cq0:cq0 + ncq], lhsT=obd[:], rhs=PT[pj][:, :ncq],
                                        start=False, stop=lastslot)
                    vg = (s_ // 8) * 8
                    S.op("pe", [("Vbd", vg, 0), ("Vbd", vg, 1), "obd", ("PT", pj)], [("ps", 6), ("ps", 7)], mmpv)
                S.op("act", [("ps", 7)], ["rt"],
                     lambda e: e.activation(out=lnd[:, :N], in_=ps[7][:, :N], func=AF.Ln))
                S.op("act", ["rt"], ["rt"],
                     lambda e: e.activation(out=lnd[:, :N], in_=lnd[:, :N], func=AF.Exp, scale=-1.0))
                S.op("dve", [("ps", 6), "rt"], ["OT"],
                     lambda e: e.tensor_tensor(out=OT[:, :N], in0=ps[6][:, :N], in1=lnd[:, :N], op=ALU.mult))

            def outproj(idx):
                j, p = items[idx]
                N = info[idx][0]
                for oc in range(NCH):
                    pb = [6, 7, 4, 5][cnt["ob"] % 4]
                    cnt["ob"] += 1
                    S.op("pe", ["wo", "OT"], [("ps", pb)],
                         lambda e, oc=oc, pb=pb: e.matmul(ps[pb][:, :N], lhsT=wo[:, oc * 128:(oc + 1) * 128],
                                                          rhs=OT[:, :N], start=True, stop=True))
                    if j < 0:
                        a_, b_, o0 = PAD, 64, PAD
                    else:
                        a_, b_, o0 = 64 + 512 * j, 64 + 512 * (j + 1), 0
                    hk = self.hkeys(oc, a_, b_)
                    S.op("dve", [("ps", pb)] + hk, hk,
                         lambda e, oc=oc, pb=pb, a_=a_, b_=b_, o0=o0: e.tensor_tensor(
                             out=self.hT[:, oc, a_:b_], in0=ps[pb][:, o0:o0 + (b_ - a_)],
                             in1=self.hT[:, oc, a_:b_], op=ALU.add))

            load_wq(0)
            load_wq(1)
            load_wo(0)
            proj(0)
            for idx in range(len(items)):
                attention(idx)
                if idx + 1 < len(items):
                    proj(idx + 1)
                    if idx + 2 < len(items):
                        load_wq(idx + 2)
                outproj(idx)
                if idx + 1 < len(items):
                    load_wo(idx + 1)

    def final(self):
        S = self.S
        nc = self.nc
        S.barrier()
        with contextlib.ExitStack() as ls:
            sbt = lambda name, shape, dtype: ls.enter_context(nc.sbuf_tensor("f_" + name, shape, dtype))
            W = self.norm_work(sbt)
            ob = [sbt("ob%d" % j, [128, NCH, 512], F32) for j in range(2)]
            for t in range(1, 9):
                j = t % 2
                c0, n = TILES[t]
                self.rmsnorm_tile(t, "final_g", ob[j], 0, ("ob", j), W)
                for c in range(NCH):
                    S.dma("sp", self.outT[c * 128:(c + 1) * 128, c0 - 64:c0 - 64 + n], ob[j][:, c, :n],
                          [("ob", j)], [("out", c, t)])
            S.barrier()


def make_vecs(VL, inp):
    v = np.zeros((128, VL.n), np.float32)

    def put(name, arr):
        o, n = VL.cols[name]
        assert arr.shape == (128, n), (name, arr.shape, n)
        v[:, o:o + n] = arr
    for i in range(DEPTH):
        put(("mix_g", i), _pc(inp["norm_mix_g"][i]))
        put(("mlp_g", i), _pc(inp["norm_mlp_g"][i]))
    put("final_g", _pc(inp["final_norm_g"]))
    for j in range(2):
        b_in = np.asarray(inp["conv_b_in"][j], np.float32)
        put(("b_in", j), np.ascontiguousarray(b_in.reshape(16, 128).T))
        put(("b_dw", j), _pc(inp["conv_b_dw"][j]))
        put(("ln_g", j), _pc(inp["conv_ln_g"][j]))
        put(("ln_b", j), _pc(inp["conv_ln_b"][j]))
        put(("b_out", j), _pc(inp["conv_b_out"][j]))
        wdw = np.asarray(inp["conv_w_dw"][j], np.float32)
        put(("w_dw", j), np.ascontiguousarray(wdw.reshape(CW, NCH, 128).transpose(2, 0, 1).reshape(128, CW * NCH)))
    return v


def make_in_maps(B, inp, ncores):
    x = np.asarray(inp["x"], np.float32)
    meta = np.asarray(inp["meta_tokens"], np.float32)
    vecs = make_vecs(B.VL, inp)
    wqkv = np.asarray(inp["na_w_qkv"], np.float32)
    wqkv_r = np.ascontiguousarray(
        wqkv.reshape(2, NCH, 128, 3, 8, 128).transpose(0, 4, 2, 1, 3, 5)).reshape(2, 8, 128, 3072)
    rpb = np.asarray(inp["na_rpb"], np.float32)
    ck = np.arange(64)[:, None]
    cq = np.arange(64)[None, :]
    dcix = np.clip(ck - cq, -15, 15) + 15
    ii = np.arange(15)
    g = rpb[:, :, 14 - ii][:, :, :, dcix]
    g = g.reshape(2, 8, 2, 15, 64, 64).transpose(0, 1, 2, 4, 3, 5)
    na_bias = np.ascontiguousarray(g).reshape(2, 8, 128, 960)
    sc = np.clip(cq - 8, 0, 48)
    valid = (ck >= sc) & (ck < sc + 16)
    nac = np.zeros((128, 65), np.float32)
    nac[:, 0:64] = np.tile(np.where(valid, 0.0, NEG).astype(np.float32), (2, 1))
    nac[:, 64] = np.tile(np.where(np.arange(64) >= PAD, 0.0, NEG).astype(np.float32), 2)
    shared = {
        "wqkv_r": wqkv_r,
        "na_w_o": np.ascontiguousarray(inp["na_w_o"], dtype=np.float32),
        "na_bias": na_bias,
        "nac": nac,
        "vecs": vecs,
        "ident": np.eye(128, dtype=np.float32),
        "mlp_w1": np.ascontiguousarray(inp["mlp_w1"], dtype=np.float32),
        "mlp_w2": np.ascontiguousarray(inp["mlp_w2"], dtype=np.float32),
        "conv_w_in": np.ascontiguousarray(inp["conv_w_in"], dtype=np.float32),
        "conv_w_out": np.ascontiguousarray(inp["conv_w_out"], dtype=np.float32),
    }
    maps = []
    for b in range(ncores):
        h0 = np.zeros((LT, D), np.float32)
        h0[PAD:PAD + NMETA] = meta
        h0[64:] = x[b]
        m = dict(shared)
        m["h0T"] = np.ascontiguousarray(h0.T)
        maps.append(m)
    return maps


def kernel(**inp):
    B = Builder()
    nc = B.build()
    ncores = 8
    maps = make_in_maps(B, inp, ncores)
    res = run_bass_kernel_spmd(nc, maps, core_ids=list(range(ncores)))
    out = np.stack([np.ascontiguousarray(res.results[b]["outT"].T) for b in range(ncores)], axis=0)
    return out.astype(np.float32)
```

```python
import contextlib
import numpy as np
import concourse.bass as bass
import concourse.mybir as mybir
from concourse.bass_utils import run_bass_kernel_spmd

F32 = mybir.dt.float32
BF16 = mybir.dt.bfloat16
AF = mybir.ActivationFunctionType
ALU = mybir.AluOpType

D = 1024
NCH = 8
DFF = 4096
SEQ = 4096
NMETA = 16
PAD = 48
LT = 64 + SEQ
DEPTH = 4
EPS = 1e-6
CW = 31
NEG = -30000.0
TILES = [(PAD, NMETA)] + [(64 + 512 * i, 512) for i in range(8)]


class Sched:
    NDMA = 8

    def __init__(self, nc, es):
        self.nc = nc
        self.engs = {"pe": nc.tensor, "dve": nc.vector, "act": nc.scalar, "pool": nc.gpsimd, "sp": nc.sync}
        self.sem = {e: es.enter_context(nc.semaphore("s_" + e)) for e in self.engs}
        self.cnt = {e: 0 for e in self.engs}
        self.semobj = {("e", e): self.sem[e] for e in self.engs}
        self.dsem = {}
        self.dval = {}
        self.dnext = {}
        for q in ("sp", "pool"):
            self.dnext[q] = 0
            for i in range(self.NDMA):
                k = ("d", q, i)
                self.semobj[k] = es.enter_context(nc.semaphore("d_%s%d" % (q, i)))
                self.dval[k] = 0
        self.seen = {e: {} for e in self.engs}
        self.last_w = {}
        self.readers = {}
        self.nwaits = 0

    def _wait(self, e, sk, v):
        if self.seen[e].get(sk, 0) >= v:
            return
        self.engs[e].wait_ge(self.semobj[sk], v)
        self.seen[e][sk] = v
        self.nwaits += 1

    def _deps(self, e, reads, writes):
        own = ("e", e)
        deps = {}

        def add(sk, v):
            if deps.get(sk, 0) < v:
                deps[sk] = v
        for k in reads:
            w = self.last_w.get(k)
            if w is not None:
                add(*w)
        for k in writes:
            w = self.last_w.get(k)
            if w is not None and w[0] != own:
                add(*w)
            for sk, v in self.readers.get(k, {}).items():
                if sk != own:
                    add(sk, v)
        for sk, v in deps.items():
            self._wait(e, sk, v)

    def _record(self, tok, reads, writes):
        sk, v = tok
        for k in writes:
            self.last_w[k] = tok
            self.readers[k] = {}
        for k in reads:
            r = self.readers.setdefault(k, {})
            if r.get(sk, 0) < v:
                r[sk] = v

    def op(self, e, reads, writes, emit):
        self._deps(e, reads, writes)
        inst = emit(self.engs[e])
        self.cnt[e] += 1
        inst.then_inc(self.sem[e], 1)
        tok = (("e", e), self.cnt[e])
        self._record(tok, reads, writes)
        return tok

    def dma(self, q, out, in_, reads, writes):
        self._deps(q, reads, writes)
        i = self.dnext[q] % self.NDMA
        self.dnext[q] += 1
        k = ("d", q, i)
        self._wait(q, k, self.dval[k])
        self.engs[q].dma_start(out=out, in_=in_).then_inc(self.semobj[k], 16)
        self.dval[k] += 16
        tok = (k, self.dval[k])
        self._record(tok, reads, writes)
        return tok

    def barrier(self):
        for e in self.engs:
            for f in self.engs:
                if f != e and self.cnt[f] > 0:
                    self._wait(e, ("e", f), self.cnt[f])
            for k, v in self.dval.items():
                if v > 0:
                    self._wait(e, k, v)


def _pc(v):
    return np.ascontiguousarray(np.asarray(v, np.float32).reshape(NCH, 128).T)


class VecLayout:
    def __init__(self):
        self.cols = {}
        self.n = 0

    def add(self, name, ncols):
        self.cols[name] = (self.n, ncols)
        self.n += ncols


def vec_layout():
    L = VecLayout()
    for i in range(DEPTH):
        L.add(("mix_g", i), 8)
        L.add(("mlp_g", i), 8)
    L.add("final_g", 8)
    for j in range(2):
        L.add(("b_in", j), 16)
        L.add(("b_dw", j), 8)
        L.add(("ln_g", j), 8)
        L.add(("ln_b", j), 8)
        L.add(("b_out", j), 8)
        L.add(("w_dw", j), CW * 8)
    return L


class Builder:
    def __init__(self, layers=None, do_final=True):
        self.layers = list(range(DEPTH)) if layers is None else layers
        self.do_final = do_final
        self.VL = vec_layout()

    def build(self):
        nc = bass.Bass("TRN2", target_bir_lowering=False)
        self.nc = nc
        es = contextlib.ExitStack()
        self.es = es
        S = Sched(nc, es)
        self.S = S
        dt = nc.dram_tensor
        self.h0T = dt("h0T", [D, LT], F32, kind="ExternalInput").ap()
        self.vecs_d = dt("vecs", [128, self.VL.n], F32, kind="ExternalInput").ap()
        self.mlp_w1 = dt("mlp_w1", [DEPTH, D, DFF], F32, kind="ExternalInput").ap()
        self.mlp_w2 = dt("mlp_w2", [DEPTH, DFF, D], F32, kind="ExternalInput").ap()
        self.conv_w_in = dt("conv_w_in", [2, D, 2 * D], F32, kind="ExternalInput").ap()
        self.conv_w_out = dt("conv_w_out", [2, D, D], F32, kind="ExternalInput").ap()
        self.ident_d = dt("ident", [128, 128], F32, kind="ExternalInput").ap()
        self.wqkv_r = dt("wqkv_r", [2, 8, 128, 3072], F32, kind="ExternalInput").ap()
        self.na_w_o = dt("na_w_o", [2, D, D], F32, kind="ExternalInput").ap()
        self.na_bias = dt("na_bias", [2, 8, 128, 960], F32, kind="ExternalInput").ap()
        self.nac_d = dt("nac", [128, 65], F32, kind="ExternalInput").ap()
        self.outT = dt("outT", [D, SEQ], F32, kind="ExternalOutput").ap()

        sb = lambda name, shape, dtype: es.enter_context(nc.sbuf_tensor(name, shape, dtype))
        self.hT = sb("hT", [128, NCH, LT], F32)
        self.vecs = sb("vecs_sb", [128, self.VL.n], F32)
        self.ones = sb("ones", [128, 128], BF16)
        self.ident = sb("ident_sb", [128, 128], BF16)
        self.ps = [es.enter_context(nc.psum_tensor("ps%d" % i, [128, 512], F32)) for i in range(8)]

        S.op("dve", [], ["ones"], lambda e: e.memset(self.ones[:], 1.0))
        S.dma("sp", self.vecs[:], self.vecs_d, [], ["vecs"])
        S.dma("pool", self.ident[:], self.ident_d, [], ["ident"])
        h0v = self.h0T.rearrange("(c p) t -> p c t", p=128)
        for (a_, b_) in [(0, 576)] + [(576 + 512 * k, 576 + 512 * (k + 1)) for k in range(7)]:
            S.dma("sp", self.hT[:, :, a_:b_], h0v[:, :, a_:b_], [],
                  [k_ for c in range(NCH) for k_ in self.hkeys(c, a_, b_)])

        for i in self.layers:
            if i % 2 == 0:
                self.conv_layer(i)
            else:
                self.na_layer(i)
            self.mlp_layer(i)
        if self.do_final:
            self.final()
        S.barrier()
        es.close()
        return nc

    def vcol(self, name, c=0, n=1):
        o, _ = self.VL.cols[name]
        return self.vecs[:, o + c:o + c + n]

    def hkeys(self, c, a, b):
        return [("h", c, k) for k in range(a // 16, (b + 15) // 16)]

    def rmsnorm_cols(self, c0, n, gname, hn, hn_off, hnkey, W):
        S = self.S
        hT = self.hT
        ssb = W["ps_ss"]
        for c in range(NCH):
            j = W["sq_i"] % 2
            W["sq_i"] += 1
            sq = W["sq"][j]
            S.op("act", self.hkeys(c, c0, c0 + n), [("sq", j)],
                 lambda e, c=c, sq=sq: e.activation(out=sq[:, :n], in_=hT[:, c, c0:c0 + n], func=AF.Square))
            S.op("pe", [("sq", j), "ones"], [("ps", ssb)],
                 lambda e, c=c, sq=sq: e.matmul(self.ps[ssb][:, :n], lhsT=self.ones[:], rhs=sq[:, :n],
                                                start=(c == 0), stop=(c == NCH - 1)))
        rt = W["rt"]
        S.op("act", [("ps", ssb), "epsc"], ["rt"],
             lambda e: e.activation(out=rt[:, :n], in_=self.ps[ssb][:, :n], func=AF.Sqrt,
                                    scale=1.0 / D, bias=W["eps"][:, 0:1]))
        S.op("dve", ["rt"], ["rstd"], lambda e: e.reciprocal(out=W["rstd"][:, :n], in_=rt[:, :n]))
        for c in range(NCH):
            S.op("dve", self.hkeys(c, c0, c0 + n) + ["rstd", "vecs"], [hnkey],
                 lambda e, c=c: e.scalar_tensor_tensor(
                     out=hn[:, c, hn_off:hn_off + n], in0=hT[:, c, c0:c0 + n],
                     scalar=self.vcol(gname, c), in1=W["rstd"][:, :n], op0=ALU.mult, op1=ALU.mult))

    def rmsnorm_tile(self, t, gname, hn, hn_off, hnkey, W):
        c0, n = TILES[t]
        self.rmsnorm_cols(c0, n, gname, hn, hn_off, hnkey, W)

    def norm_work(self, sbt, nmax=512):
        W = {"sq": [sbt("sq%d" % j, [128, nmax], BF16) for j in range(2)], "sq_i": 0,
             "rt": sbt("rt", [128, nmax], F32), "rstd": sbt("rstd", [128, nmax], F32),
             "eps": sbt("epsc", [128, 1], F32), "ps_ss": 7}
        self.S.op("dve", [], ["epsc"], lambda e: e.memset(W["eps"][:], EPS))
        return W

    def mlp_layer(self, i):
        S = self.S
        nc = self.nc
        S.barrier()
        with contextlib.ExitStack() as ls:
            sbt = lambda name, shape, dtype: ls.enter_context(nc.sbuf_tensor("m%d_%s" % (i, name), shape, dtype))
            W = self.norm_work(sbt)
            NSLOT = 3
            hn = sbt("hn", [128, NCH, NSLOT * 512], BF16)
            w1 = [sbt("w1_%d" % j, [128, NCH, 512], BF16) for j in range(2)]
            w2 = [sbt("w2_%d" % j, [128, 4, D], BF16) for j in range(2)]
            ut = [sbt("ut%d" % j, [128, 4, 512], BF16) for j in range(2)]
            rl = [sbt("rl%d" % j, [128, 512], F32) for j in range(2)]
            blocks = [[0, 1, 2], [3, 4, 5], [6, 7, 8]]
            w1d = self.mlp_w1[i].rearrange("(kc p) f -> p kc f", p=128)
            w2d = self.mlp_w2[i].rearrange("(fc p) o -> p fc o", p=128)
            gi = 0

            def load_w(g, j):
                S.dma("pool", w1[j][:], w1d[:, :, g * 512:(g + 1) * 512], [], [("w1", j)])
                S.dma("pool", w2[j][:], w2d[:, g * 4:(g + 1) * 4, :], [], [("w2", j)])

            seq = [(b, g) for b in range(len(blocks)) for g in range(8)]
            load_w(seq[0][1], 0)
            hid_i = 0
            out_i = 0
            ut_i = 0
            for si, (b, g) in enumerate(seq):
                blk = blocks[b]
                if g == 0:
                    for s, t in enumerate(blk):
                        self.rmsnorm_tile(t, ("mlp_g", i), hn, s * 512, ("hn", s), W)
                j = si % 2
                if si + 1 < len(seq):
                    load_w(seq[si + 1][1], (si + 1) % 2)

                def hid(s, t, uj):
                    nonlocal hid_i
                    c0, n = TILES[t]
                    for fc in range(4):
                        pb = hid_i % 2
                        hid_i += 1
                        def mm(e, fc=fc, pb=pb):
                            last = None
                            for kc in range(NCH):
                                last = e.matmul(self.ps[pb][:, :n], lhsT=w1[j][:, kc, fc * 128:(fc + 1) * 128],
                                                rhs=hn[:, kc, s * 512:s * 512 + n], start=(kc == 0), stop=(kc == NCH - 1))
                            return last
                        S.op("pe", [("w1", j), ("hn", s)], [("ps", pb)], mm)
                        S.op("act", [("ps", pb)], [("rl", pb)],
                             lambda e, pb=pb: e.activation(out=rl[pb][:, :n], in_=self.ps[pb][:, :n], func=AF.Relu))
                        S.op("act", [("rl", pb)], [("ut", uj, fc)],
                             lambda e, pb=pb, fc=fc: e.activation(out=ut[uj][:, fc, :n], in_=rl[pb][:, :n], func=AF.Square))

                def outp(s, t, uj):
                    nonlocal out_i
                    c0, n = TILES[t]
                    for oc in range(NCH):
                        pb = 2 + out_i % 4
                        out_i += 1
                        def mm(e, oc=oc, pb=pb):
                            last = None
                            for fc in range(4):
                                last = e.matmul(self.ps[pb][:, :n], lhsT=w2[j][:, fc, oc * 128:(oc + 1) * 128],
                                                rhs=ut[uj][:, fc, :n], start=(fc == 0), stop=(fc == 3))
                            return last
                        S.op("pe", [("w2", j)] + [("ut", uj, fc) for fc in range(4)], [("ps", pb)], mm)
                        S.op("dve", [("ps", pb)] + self.hkeys(oc, c0, c0 + n), self.hkeys(oc, c0, c0 + n),
                             lambda e, oc=oc, pb=pb: e.tensor_tensor(
                                 out=self.hT[:, oc, c0:c0 + n], in0=self.ps[pb][:, :n],
                                 in1=self.hT[:, oc, c0:c0 + n], op=ALU.add))

                pend = None
                for s, t in enumerate(blk):
                    uj = ut_i % 2
                    ut_i += 1
                    hid(s, t, uj)
                    if pend is not None:
                        outp(*pend)
                    pend = (s, t, uj)
                outp(*pend)

    def rmsnorm_gen(self, c0, n, gname, hn, hn_off, hnkey, W):
        S = self.S
        hT = self.hT
        ssb = W["ps_ss"]
        for c in range(NCH):
            j = W["sq_i"] % 2
            W["sq_i"] += 1
            sq = W["sq"][j]
            S.op("act", self.hkeys(c, c0, c0 + n), [("sq", j)],
                 lambda e, c=c, sq=sq: e.activation(out=sq[:, :n], in_=hT[:, c, c0:c0 + n], func=AF.Square))
            S.op("pe", [("sq", j), "ones"], [("ps", ssb)],
                 lambda e, c=c, sq=sq: e.matmul(self.ps[ssb][:, :n], lhsT=self.ones[:], rhs=sq[:, :n],
                                                start=(c == 0), stop=(c == NCH - 1)))
            yield
        rt = W["rt"]
        S.op("act", [("ps", ssb), "epsc"], ["rt"],
             lambda e: e.activation(out=rt[:, :n], in_=self.ps[ssb][:, :n], func=AF.Sqrt,
                                    scale=1.0 / D, bias=W["eps"][:, 0:1]))
        S.op("dve", ["rt"], ["rstd"], lambda e: e.reciprocal(out=W["rstd"][:, :n], in_=rt[:, :n]))
        yield
        for c in range(NCH):
            S.op("dve", self.hkeys(c, c0, c0 + n) + ["rstd", "vecs"], [hnkey],
                 lambda e, c=c: e.scalar_tensor_tensor(
                     out=hn[:, c, hn_off:hn_off + n], in0=hT[:, c, c0:c0 + n],
                     scalar=self.vcol(gname, c), in1=W["rstd"][:, :n], op0=ALU.mult, op1=ALU.mult))
            yield

    @staticmethod
    def interleave(*gens, ratio=None):
        gens = [g for g in gens if g is not None]
        alive = list(gens)
        while alive:
            for g in list(alive):
                try:
                    next(g)
                except StopIteration:
                    alive.remove(g)

    def conv_layer(self, i):
        S = self.S
        nc = self.nc
        jl = i // 2
        S.barrier()
        N = 256
        with contextlib.ExitStack() as ls:
            sbt = lambda name, shape, dtype: ls.enter_context(nc.sbuf_tensor("c%d_%s" % (i, name), shape, dtype))
            W = self.norm_work(sbt, N)
            W["ps_ss"] = 3
            Gb = sbt("Gb", [128, NCH, 30 + N], BF16)
            ctmp = sbt("ctmp", [128, NCH, 30], BF16)
            hn = sbt("hn", [128, NCH, N], BF16)
            sT = sbt("sT", [128, NCH, N], BF16)
            y = sbt("y", [128, NCH, N], F32)
            ybf = [sbt("ybf%d" % k, [128, N], BF16) for k in range(2)]
            ysq = [sbt("ysq%d" % k, [128, N], BF16) for k in range(2)]
            dg = [sbt("dg%d" % k, [128, 8, 128], BF16) for k in range(4)]
            mu = sbt("mu", [128, N], F32)
            vb = sbt("vb", [128, N], F32)
            rs = sbt("rs", [128, N], F32)
            nm = sbt("nm", [128, N], F32)
            sg = [sbt("sg%d" % k, [128, N], F32) for k in range(2)]
            sgz = [sbt("sgz%d" % k, [128, N], F32) for k in range(2)]
            NWB = 4
            wb = [sbt("wb%d" % k, [128, NCH, 512], BF16) for k in range(NWB)]
            win = self.conv_w_in[jl].rearrange("(kc p) f -> p kc f", p=128)
            wout = self.conv_w_out[jl].rearrange("(kc p) f -> p kc f", p=128)
            steps = [(PAD + 242 * k, 242) for k in range(16)] + [(PAD + 242 * 16, 240)]
            assert steps[-1][0] + steps[-1][1] == LT
            nreal = len(steps)
            LAST = len(steps) - 1
            inl = [win[:, :, 0:512], win[:, :, 1024:1536], win[:, :, 512:1024], win[:, :, 1536:2048]]
            outl = [wout[:, :, 0:512], wout[:, :, 512:1024]]
            loads = list(inl)
            for st in range(len(steps)):
                if st >= 1:
                    loads += outl
                if st + 1 < nreal:
                    loads += inl
            loads += outl
            issued = [0]
            consumed = [0]

            def ensure(idx):
                while issued[0] <= min(idx, len(loads) - 1):
                    k = issued[0]
                    S.dma("pool", wb[k % NWB][:], loads[k], [], [("wb", k % NWB)])
                    issued[0] += 1
            lptr = [0]

            def next_w():
                k = lptr[0]
                lptr[0] += 1
                ensure(k)
                return k % NWB

            def done_w(nloads):
                consumed[0] += nloads
                ensure(consumed[0] + NWB - 1)

            S.op("dve", [], [("G", c) for c in range(NCH)], lambda e: e.memset(Gb[:, :, 0:30], 0.0))
            cs = {"inb": 0, "obk": 0, "yi": 0}

            def geom(st):
                c0, n = steps[st]
                m0 = 15 if st == 0 else 0
                nn = n - m0 + (15 if st == LAST else 0)
                a = c0 - 15 + m0
                return c0, n, m0, nn, a, a + nn

            def rms_gen(st):
                if st < nreal:
                    c0, n = steps[st]
                    yield from self.rmsnorm_gen(c0, n, ("mix_g", i), hn, 0, "hnc", W)

            def inproj_gen(st):
                c0, n, m0, nn, a, b = geom(st)
                if st == LAST:
                    S.op("dve", [], [("G", c) for c in range(NCH)],
                         lambda e: e.memset(Gb[:, :, 30 + n:30 + n + 16], 0.0))
                for hf in range(2):
                    ba = next_w()
                    bg = next_w()
                    for q in range(4):
                        oc = hf * 4 + q
                        pa = cs["inb"] % 4
                        cs["inb"] += 1
                        def mmag(e, q=q, pa=pa, ba=ba, bg=bg):
                            last = None
                            for kc in range(NCH):
                                last = e.matmul(self.ps[pa][:, 0:n], lhsT=wb[ba][:, kc, q * 128:(q + 1) * 128],
                                                rhs=hn[:, kc, :n], start=(kc == 0), stop=(kc == NCH - 1))
                            for kc in range(NCH):
                                last = e.matmul(self.ps[pa][:, 256:256 + n], lhsT=wb[bg][:, kc, q * 128:(q + 1) * 128],
                                                rhs=hn[:, kc, :n], start=(kc == 0), stop=(kc == NCH - 1))
                            return last
                        S.op("pe", [("wb", ba), ("wb", bg), "hnc"], [("ps", pa)], mmag)
                        sgi = cs["inb"] % 2
                        S.op("act", [("ps", pa), "vecs"], [("sg", sgi)],
                             lambda e, pa=pa, oc=oc, sgi=sgi: e.activation(
                                 out=sg[sgi][:, :n], in_=self.ps[pa][:, 256:256 + n], func=AF.Sigmoid,
                                 bias=self.vcol(("b_in", jl), 8 + oc)))
                        S.op("dve", [("ps", pa), ("sg", sgi), "vecs"], [("G", oc)],
                             lambda e, pa=pa, oc=oc, sgi=sgi: e.scalar_tensor_tensor(
                                 out=Gb[:, oc, 30:30 + n], in0=self.ps[pa][:, 0:n],
                                 scalar=self.vcol(("b_in", jl), oc), in1=sg[sgi][:, :n],
                                 op0=ALU.add, op1=ALU.mult))
                        yield
                    done_w(2)

            pending = [None]

            def builds(c):
                for qt in range(4):
                    for k in range(8 * qt, min(8 * qt + 8, CW)):
                        wcol = self.vcol(("w_dw", jl), k * 8 + c)
                        dkey = ("dg", qt, k % 8)
                        if k % 8 in (3, 7):
                            S.op("act", ["ident", "vecs"], [dkey],
                                 lambda e, k=k, qt=qt, wcol=wcol: e.activation(
                                     out=dg[qt][:, k % 8, :], in_=self.ident[:], func=AF.Copy, scale=wcol))
                        elif k % 8 in (1, 5):
                            S.op("pool", ["ident", "vecs"], [dkey],
                                 lambda e, k=k, qt=qt, wcol=wcol: e.tensor_scalar(
                                     out=dg[qt][:, k % 8, :], in0=self.ident[:], scalar1=wcol, scalar2=0.0,
                                     op0=ALU.mult, op1=ALU.add))
                        else:
                            S.op("dve", ["ident", "vecs"], [dkey],
                                 lambda e, k=k, qt=qt, wcol=wcol: e.tensor_scalar(
                                     out=dg[qt][:, k % 8, :], in0=self.ident[:], scalar1=wcol, scalar2=None,
                                     op0=ALU.mult))

            def conv_gen(st):
                c0, n, m0, nn, a, b = geom(st)
                for c in range(NCH):
                    for qt in range(4):
                        ks = list(range(8 * qt, min(8 * qt + 8, CW)))
                        pb = 4 + c % 2
                        def mmc(e, c=c, ks=ks, qt=qt, pb=pb):
                            last = None
                            for k in ks:
                                last = e.matmul(self.ps[pb][:, :nn], lhsT=dg[qt][:, k % 8, :],
                                                rhs=Gb[:, c, m0 + k:m0 + k + nn], start=(k == 0), stop=(k == CW - 1))
                            return last
                        S.op("pe", [("dg", qt, k % 8) for k in ks] + [("G", c)], [("ps", pb)], mmc)
                    if not (st == len(steps) - 1 and c == NCH - 1):
                        builds((c + 1) % NCH)
                    if pending[0] is not None:
                        pending[0]()
                        pending[0] = None
                    pb = 4 + c % 2
                    bdw = self.vcol(("b_dw", jl), c)
                    yj = cs["yi"] % 2
                    cs["yi"] += 1
                    S.op("act", [("ps", pb), "vecs"], [("y", c)],
                         lambda e, c=c, pb=pb, bdw=bdw: e.activation(out=y[:, c, :nn], in_=self.ps[pb][:, :nn],
                                                                     func=AF.Identity, bias=bdw))
                    S.op("act", [("ps", pb), "vecs"], [("ybf", yj)],
                         lambda e, pb=pb, bdw=bdw, yj=yj: e.activation(out=ybf[yj][:, :nn], in_=self.ps[pb][:, :nn],
                                                                       func=AF.Identity, bias=bdw))
                    S.op("act", [("ps", pb), "vecs"], [("ysq", yj)],
                         lambda e, pb=pb, bdw=bdw, yj=yj: e.activation(out=ysq[yj][:, :nn], in_=self.ps[pb][:, :nn],
                                                                       func=AF.Square, bias=bdw))
                    def stats(c=c, yj=yj):
                        S.op("pe", [("ybf", yj), "ones"], [("ps", 6)],
                             lambda e: e.matmul(self.ps[6][:, :nn], lhsT=self.ones[:], rhs=ybf[yj][:, :nn],
                                                start=(c == 0), stop=(c == NCH - 1)))
                        S.op("pe", [("ysq", yj), "ones"], [("ps", 7)],
                             lambda e: e.matmul(self.ps[7][:, :nn], lhsT=self.ones[:], rhs=ysq[yj][:, :nn],
                                                start=(c == 0), stop=(c == NCH - 1)))
                    pending[0] = stats
                    yield
                pending[0]()
                pending[0] = None
                if st != LAST:
                    S.op("dve", [("G", c) for c in range(NCH)], ["ctmp"],
                         lambda e: e.tensor_copy(out=ctmp[:], in_=Gb[:, :, n:n + 30]))
                    S.op("dve", ["ctmp"], [("G", c) for c in range(NCH)],
                         lambda e: e.tensor_copy(out=Gb[:, :, 0:30], in_=ctmp[:]))

            def ln_gen(st):
                c0, n, m0, nn, a, b = geom(st)
                S.op("act", [("ps", 6)], ["mu"],
                     lambda e: e.activation(out=mu[:, :nn], in_=self.ps[6][:, :nn], func=AF.Copy, scale=1.0 / D))
                S.op("dve", ["mu"], ["vb"],
                     lambda e: e.tensor_tensor(out=vb[:, :nn], in0=mu[:, :nn], in1=mu[:, :nn], op=ALU.mult))
                S.op("dve", [("ps", 7), "vb"], ["vb"],
                     lambda e: e.scalar_tensor_tensor(out=vb[:, :nn], in0=self.ps[7][:, :nn], scalar=1.0 / D,
                                                      in1=vb[:, :nn], op0=ALU.mult, op1=ALU.subtract))
                yield
                S.op("act", ["vb", "epsc"], ["vb"],
                     lambda e: e.activation(out=vb[:, :nn], in_=vb[:, :nn], func=AF.Sqrt, bias=W["eps"][:, 0:1]))
                S.op("dve", ["vb"], ["rs"], lambda e: e.reciprocal(out=rs[:, :nn], in_=vb[:, :nn]))
                S.op("dve", ["mu", "rs"], ["nm"],
                     lambda e: e.scalar_tensor_tensor(out=nm[:, :nn], in0=mu[:, :nn], scalar=-1.0, in1=rs[:, :nn],
                                                      op0=ALU.mult, op1=ALU.mult))
                yield

            def ln_chunks_gen(st):
                c0, n, m0, nn, a, b = geom(st)
                for c in range(NCH):
                    S.op("dve", [("y", c), "rs"], [("y", c)],
                         lambda e, c=c: e.tensor_tensor(out=y[:, c, :nn], in0=y[:, c, :nn], in1=rs[:, :nn], op=ALU.mult))
                    S.op("dve", [("y", c), "nm"], [("y", c)],
                         lambda e, c=c: e.tensor_tensor(out=y[:, c, :nn], in0=y[:, c, :nn], in1=nm[:, :nn], op=ALU.add))
                    zi = c % 2
                    S.op("act", [("y", c), "vecs"], [("sgz", zi)],
                         lambda e, c=c, zi=zi: e.activation(out=sgz[zi][:, :nn], in_=y[:, c, :nn], func=AF.Sigmoid,
                                                            scale=self.vcol(("ln_g", jl), c),
                                                            bias=self.vcol(("ln_b", jl), c)))
                    S.op("act", [("y", c), "vecs"], [("y", c)],
                         lambda e, c=c: e.activation(out=y[:, c, :nn], in_=y[:, c, :nn], func=AF.Identity,
                                                     scale=self.vcol(("ln_g", jl), c), bias=self.vcol(("ln_b", jl), c)))
                    S.op("pool", [("y", c), ("sgz", zi)], [("sT", c)],
                         lambda e, c=c, zi=zi: e.tensor_tensor(out=sT[:, c, :nn], in0=y[:, c, :nn],
                                                               in1=sgz[zi][:, :nn], op=ALU.mult))
                    yield

            def outproj_gen(st):
                c0, n, m0, nn, a, b = geom(st)
                for hf in range(2):
                    bo = next_w()
                    for q in range(4):
                        oc = hf * 4 + q
                        pb = cs["obk"] % 4
                        cs["obk"] += 1
                        def mmo(e, q=q, pb=pb, bo=bo):
                            last = None
                            for kc in range(NCH):
                                last = e.matmul(self.ps[pb][:, :nn], lhsT=wb[bo][:, kc, q * 128:(q + 1) * 128],
                                                rhs=sT[:, kc, :nn], start=(kc == 0), stop=(kc == NCH - 1))
                            return last
                        S.op("pe", [("wb", bo)] + [("sT", c) for c in range(NCH)], [("ps", pb)], mmo)
                        hk = self.hkeys(oc, a, b)
                        S.op("dve", [("ps", pb), "vecs"] + hk, hk,
                             lambda e, oc=oc, pb=pb: e.scalar_tensor_tensor(
                                 out=self.hT[:, oc, a:b], in0=self.ps[pb][:, :nn], scalar=self.vcol(("b_out", jl), oc),
                                 in1=self.hT[:, oc, a:b], op0=ALU.add, op1=ALU.add))
                        yield
                    done_w(1)

            def seq(*gens):
                for g in gens:
                    if g is not None:
                        yield from g

            ensure(NWB - 1)
            self.interleave(rms_gen(0))
            builds(0)
            self.interleave(inproj_gen(0))
            for st in range(len(steps)):
                nxt = st + 1 < len(steps)
                cg = conv_gen(st)
                rg = rms_gen(st + 1) if nxt else iter(())
                lc = ln_chunks_gen(st - 1) if st >= 1 else iter(())
                for _c in range(4):
                    next(lc, None)
                    next(lc, None)
                    next(cg)
                    for _k in range(5):
                        next(rg, None)
                for _ in rg:
                    pass
                if st >= 1:
                    for _ in outproj_gen(st - 1):
                        pass
                for _ in cg:
                    pass
                for _ in ln_gen(st):
                    pass
                if nxt:
                    for _ in inproj_gen(st + 1):
                        pass
            for _ in ln_chunks_gen(len(steps) - 1):
                pass
            for _ in outproj_gen(len(steps) - 1):
                pass

    def na_layer(self, i):
        S = self.S
        nc = self.nc
        jl = i // 2
        S.barrier()
        with contextlib.ExitStack() as ls:
            sbt = lambda name, shape, dtype: ls.enter_context(nc.sbuf_tensor("a%d_%s" % (i, name), shape, dtype))
            W = self.norm_work(sbt, 512)
            T = sbt("T", [128, 8, 960], BF16)
            nac = sbt("nac", [128, 65], F32)
            zer = sbt("zer", [128, 128], BF16)
            obd = sbt("obd", [128, 128], BF16)
            S.dma("sp", nac[:], self.nac_d, [], ["nac"])
            S.op("dve", [], ["zer"], lambda e: e.memset(zer[:], 0.0))
            S.op("dve", [], ["obd"], lambda e: e.memset(obd[:], 0.0))
            S.op("dve", ["obd"], ["obd"], lambda e: e.memset(obd[0:64, 0:64], 1.0))
            S.op("dve", ["obd"], ["obd"], lambda e: e.memset(obd[64:128, 64:128], 1.0))
            with contextlib.ExitStack() as ls2:
                stg = [ls2.enter_context(nc.sbuf_tensor("a%d_stg%d" % (i, k), [128, 960], F32)) for k in range(2)]
                for pr in range(8):
                    S.dma("sp", stg[pr % 2][:], self.na_bias[jl, pr], [], [("stg", pr % 2)])
                    for ii in range(15):
                        S.op("dve", [("stg", pr % 2), "nac"], [("T", pr)],
                             lambda e, pr=pr, ii=ii: e.tensor_tensor(
                                 out=T[:, pr, ii * 64:(ii + 1) * 64], in0=stg[pr % 2][:, ii * 64:(ii + 1) * 64],
                                 in1=nac[:, 0:64], op=ALU.add))
                S.barrier()
            hnr = sbt("hnr", [128, NCH, 3 * 512], BF16)
            hnm = sbt("hnm", [128, NCH, 64], BF16)
            wq = [sbt("wq%d" % k, [128, NCH, 384], BF16) for k in range(2)]
            wo = sbt("wo", [128, D], BF16)
            qT = sbt("qT", [128, 512], BF16)
            Kbd = sbt("Kbd", [128, 16, 128], BF16)
            Vbd = sbt("Vbd", [128, 16, 128], BF16)
            PT = [sbt("PT%d" % k, [128, 512], BF16) for k in range(2)]
            OT = sbt("OT", [128, 512], BF16)
            S.op("dve", [], ["Kbd"], lambda e: e.memset(Kbd[:], 0.0))
            S.op("dve", [], [("Vbd", 0, 0), ("Vbd", 0, 1), ("Vbd", 8, 0), ("Vbd", 8, 1)],
                 lambda e: e.memset(Vbd[:], 0.0))
            ps = self.ps

            def dup(ap):
                return bass.AP(ap.tensor, ap.offset, (tuple(ap.ap[0]), (0, 2), (1, 64)))

            def sr(rq):
                return min(max(rq - 4, 0), 56)

            self.rmsnorm_cols(0, 64, ("mix_g", i), hnm, 0, "hnm", W)
            self.rmsnorm_tile(1, ("mix_g", i), hnr, 512, ("hnr", 1), W)
            self.rmsnorm_tile(2, ("mix_g", i), hnr, 1024, ("hnr", 2), W)

            def rowsrc(rk):
                tk = rk // 8 + 1
                return ("hnr", tk % 3), (tk % 3) * 512 + (rk % 8) * 64

            items = [(-1, p) for p in range(8)] + [(j, p) for j in range(8) for p in range(8)]
            wqd = self.wqkv_r[jl]
            lnd = W["rt"]

            def load_wq(idx):
                S.dma("pool", wq[idx % 2][:], wqd[items[idx][1]].rearrange("p (kc f) -> p kc f", kc=NCH),
                      [], [("wq", idx % 2)])

            def load_wo(idx):
                p = items[idx][1]
                S.dma("pool", wo[:], self.na_w_o[jl][p * 128:(p + 1) * 128, :], [], ["wo"])
            cnt = {"pt": 0, "sb": 0, "ob": 0, "kb": 0}
            info = {}

            def proj(idx):
                j, p = items[idx]
                wb_ = wq[idx % 2]
                wk = ("wq", idx % 2)
                if j < 0:
                    N = 64
                    qkey, qoff, qsrc = "hnm", 0, hnm
                    rows = []
                    rqs = []
                else:
                    N = 512
                    tq = j + 1
                    if p == 0 and tq + 1 <= 8 and tq + 1 > 2:
                        self.rmsnorm_tile(tq + 1, ("mix_g", i), hnr, ((tq + 1) % 3) * 512, ("hnr", (tq + 1) % 3), W)
                    qkey, qoff, qsrc = ("hnr", tq % 3), (tq % 3) * 512, hnr
                    rqs = list(range(8 * j, 8 * j + 8))
                    lo = min(sr(r) for r in rqs)
                    hi = max(sr(r) + 7 for r in rqs)
                    rows = list(range(lo, hi + 1))
                nslots = 1 + len(rows)
                info[idx] = (N, rows, rqs, nslots)
                def mmq(e):
                    last = None
                    for kc in range(NCH):
                        last = e.matmul(ps[0][:, :N], lhsT=wb_[:, kc, 0:128], rhs=qsrc[:, kc, qoff:qoff + N],
                                        start=(kc == 0), stop=(kc == NCH - 1))
                    return last
                S.op("pe", [wk, qkey], [("ps", 0)], mmq)
                S.op("act", [("ps", 0)], ["qT"],
                     lambda e: e.activation(out=qT[:, :N], in_=ps[0][:, :N], func=AF.Copy, scale=0.125))
                segs = [("hnm", hnm, 0, 0, 1)]
                r = 0
                while r < len(rows):
                    rk = rows[r]
                    tk = rk // 8 + 1
                    nr = 1
                    while r + nr < len(rows) and (rows[r + nr] // 8 + 1) == tk:
                        nr += 1
                    key, off = rowsrc(rk)
                    segs.append((key, hnr, off, 1 + r, nr))
                    r += nr
                for (key, src, off, s0, nr) in segs:
                    pb = 1 + cnt["kb"] % 2
                    cnt["kb"] += 1
                    ncol = nr * 64
                    def mmk(e, src=src, off=off, ncol=ncol, pb=pb):
                        last = None
                        for kc in range(NCH):
                            last = e.matmul(ps[pb][:, :ncol], lhsT=wb_[:, kc, 128:256], rhs=src[:, kc, off:off + ncol],
                                            start=(kc == 0), stop=(kc == NCH - 1))
                        return last
                    S.op("pe", [wk, key], [("ps", pb)], mmk)
                    S.op("dve", [("ps", pb)], ["Kbd"],
                         lambda e, pb=pb, s0=s0, nr=nr, ncol=ncol: e.tensor_copy(
                             out=Kbd[0:64, s0:s0 + nr, 0:64],
                             in_=ps[pb][0:64, 0:ncol].rearrange("p (r c) -> p r c", c=64)))
                    S.op("act", [("ps", pb)], ["Kbd"],
                         lambda e, pb=pb, s0=s0, nr=nr, ncol=ncol: e.activation(
                             out=Kbd[64:128, s0:s0 + nr, 64:128],
                             in_=ps[pb][64:128, 0:ncol].rearrange("p (r c) -> p r c", c=64), func=AF.Copy))
                vsrc = [("hnm", hnm, 0)] + [(rowsrc(rk)[0], hnr, rowsrc(rk)[1]) for rk in rows]
                for g0 in range(0, nslots, 8):
                    g = min(8, nslots - g0)
                    vb_ = 3 if g0 == 0 else 0
                    keys = list({vsrc[g0 + q][0] for q in range(g)})
                    def mmv(e, g0=g0, g=g, vb_=vb_):
                        last = None
                        for q in range(g):
                            _, src, off = vsrc[g0 + q]
                            for kc in range(NCH):
                                e.matmul(ps[vb_][0:64, q * 64:(q + 1) * 64], lhsT=src[:, kc, off:off + 64],
                                         rhs=wb_[:, kc, 256:320], start=(kc == 0), stop=(kc == NCH - 1))
                                last = e.matmul(ps[vb_][64:128, q * 64:(q + 1) * 64], lhsT=src[:, kc, off:off + 64],
                                                rhs=wb_[:, kc, 320:384], start=(kc == 0), stop=(kc == NCH - 1))
                        return last
                    S.op("pe", [wk] + keys, [("ps", vb_)], mmv)
                    S.op("dve", [("ps", vb_)], [("Vbd", g0, 0)],
                         lambda e, g0=g0, g=g, vb_=vb_: e.tensor_copy(
                             out=Vbd[0:64, g0:g0 + g, 0:64],
                             in_=ps[vb_][0:64, 0:g * 64].rearrange("p (q f) -> p q f", f=64)))
                    S.op("act", [("ps", vb_)], [("Vbd", g0, 1)],
                         lambda e, g0=g0, g=g, vb_=vb_: e.activation(
                             out=Vbd[64:128, g0:g0 + g, 64:128],
                             in_=ps[vb_][64:128, 0:g * 64].rearrange("p (q f) -> p q f", f=64), func=AF.Copy))

            def attention(idx):
                j, p = items[idx]
                N, rows, rqs, nslots = info[idx]
                geo = []
                for s_ in range(nslots):
                    if s_ == 0:
                        geo.append((0, N, None))
                    else:
                        rk = rows[s_ - 1]
                        att = [rq for rq in rqs if sr(rq) <= rk <= sr(rq) + 7]
                        qa, qb = att[0], att[-1]
                        assert att == list(range(qa, qb + 1))
                        i0 = 7 - rk + qa
                        assert 0 <= i0 and i0 + (qb - qa) <= 14
                        geo.append(((qa - 8 * j) * 64, (qb - qa + 1) * 64, i0))
                sbank = {}

                SB = [4, 5, 1]

                def smm(s_):
                    cq0, ncq, i0 = geo[s_]
                    pb = SB[cnt["sb"] % 3]
                    cnt["sb"] += 1
                    sbank[s_] = pb
                    S.op("pe", ["Kbd", "qT"], [("ps", pb)],
                         lambda e: e.matmul(ps[pb][:, :ncq], lhsT=Kbd[:, s_, :], rhs=qT[:, cq0:cq0 + ncq],
                                            start=True, stop=True))
                    if i0 is not None:
                        S.op("dve", [("ps", pb), ("T", p)], [("ps", pb)],
                             lambda e: e.tensor_tensor(out=ps[pb][:, :ncq], in0=ps[pb][:, :ncq],
                                                       in1=T[:, p, i0 * 64:i0 * 64 + ncq], op=ALU.add))
                smm(0)
                if nslots > 1:
                    smm(1)
                for s_ in range(nslots):
                    if s_ + 2 < nslots:
                        smm(s_ + 2)
                    cq0, ncq, i0 = geo[s_]
                    pb = sbank[s_]
                    pj = cnt["pt"] % 2
                    cnt["pt"] += 1
                    if s_ == 0:
                        S.op("act", [("ps", pb), "nac"], [("PT", pj)],
                             lambda e, pb=pb, pj=pj, ncq=ncq: e.activation(out=PT[pj][:, :ncq], in_=ps[pb][:, :ncq],
                                                                           func=AF.Exp, bias=nac[:, 64:65]))
                    else:
                        S.op("act", [("ps", pb)], [("PT", pj)],
                             lambda e, pb=pb, pj=pj, ncq=ncq: e.activation(out=PT[pj][:, :ncq], in_=ps[pb][:, :ncq],
                                                                           func=AF.Exp))
                    lastslot = s_ == nslots - 1
                    def mmpv(e, s_=s_, cq0=cq0, ncq=ncq, pj=pj, lastslot=lastslot):
                        e.matmul(ps[6][:, cq0:cq0 + ncq], lhsT=Vbd[:, s_, :], rhs=PT[pj][:, :ncq],
                                 start=(s_ == 0), stop=lastslot)
                        return e.matmul(ps[7][:, cq0:cq0 + ncq], lhsT=obd[:], rhs=PT[pj][:, :ncq],
                                        start=(s_ == 0), stop=lastslot)
                    vg = (s_ // 8) * 8
                    S.op("pe", [("Vbd", vg, 0), ("Vbd", vg, 1), "obd", ("PT", pj)], [("ps", 6), ("ps", 7)], mmpv)
                S.op("act", [("ps", 7)], ["rt"],
                     lambda e: e.activation(out=lnd[:, :N], in_=ps[7][:, :N], func=AF.Ln))
                S.op("act", ["rt"], ["rt"],
                     lambda e: e.activation(out=lnd[:, :N], in_=lnd[:, :N], func=AF.Exp, scale=-1.0))
                S.op("dve", [("ps", 6), "rt"], ["OT"],
                     lambda e: e.tensor_tensor(out=OT[:, :N], in0=ps[6][:, :N], in1=lnd[:, :N], op=ALU.mult))

            def outproj(idx):
                j, p = items[idx]
                N = info[idx][0]
                for oc in range(NCH):
                    pb = [6, 7, 4, 5][cnt["ob"] % 4]
                    cnt["ob"] += 1
                    S.op("pe", ["wo", "OT"], [("ps", pb)],
                         lambda e, oc=oc, pb=pb: e.matmul(ps[pb][:, :N], lhsT=wo[:, oc * 128:(oc + 1) * 128],
                                                          rhs=OT[:, :N], start=True, stop=True))
                    if j < 0:
                        a_, b_, o0 = PAD, 64, PAD
                    else:
                        a_, b_, o0 = 64 + 512 * j, 64 + 512 * (j + 1), 0
                    hk = self.hkeys(oc, a_, b_)
                    S.op("dve", [("ps", pb)] + hk, hk,
                         lambda e, oc=oc, pb=pb, a_=a_, b_=b_, o0=o0: e.tensor_tensor(
                             out=self.hT[:, oc, a_:b_], in0=ps[pb][:, o0:o0 + (b_ - a_)],
                             in1=self.hT[:, oc, a_:b_], op=ALU.add))

            load_wq(0)
            load_wq(1)
            load_wo(0)
            proj(0)
            for idx in range(len(items)):
                attention(idx)
                if idx + 1 < len(items):
                    proj(idx + 1)
                    if idx + 2 < len(items):
                        load_wq(idx + 2)
                outproj(idx)
                if idx + 1 < len(items):
                    load_wo(idx + 1)

    def final(self):
        S = self.S
        nc = self.nc
        S.barrier()
        with contextlib.ExitStack() as ls:
            sbt = lambda name, shape, dtype: ls.enter_context(nc.sbuf_tensor("f_" + name, shape, dtype))
            W = self.norm_work(sbt)
            ob = [sbt("ob%d" % j, [128, NCH, 512], F32) for j in range(2)]
            for t in range(1, 9):
                j = t % 2
                c0, n = TILES[t]
                self.rmsnorm_tile(t, "final_g", ob[j], 0, ("ob", j), W)
                for c in range(NCH):
                    S.dma("sp", self.outT[c * 128:(c + 1) * 128, c0 - 64:c0 - 64 + n], ob[j][:, c, :n],
                          [("ob", j)], [("out", c, t)])
            S.barrier()


def make_vecs(VL, inp):
    v = np.zeros((128, VL.n), np.float32)

    def put(name, arr):
        o, n = VL.cols[name]
        assert arr.shape == (128, n), (name, arr.shape, n)
        v[:, o:o + n] = arr
    for i in range(DEPTH):
        put(("mix_g", i), _pc(inp["norm_mix_g"][i]))
        put(("mlp_g", i), _pc(inp["norm_mlp_g"][i]))
    put("final_g", _pc(inp["final_norm_g"]))
    for j in range(2):
        b_in = np.asarray(inp["conv_b_in"][j], np.float32)
        put(("b_in", j), np.ascontiguousarray(b_in.reshape(16, 128).T))
        put(("b_dw", j), _pc(inp["conv_b_dw"][j]))
        put(("ln_g", j), _pc(inp["conv_ln_g"][j]))
        put(("ln_b", j), _pc(inp["conv_ln_b"][j]))
        put(("b_out", j), _pc(inp["conv_b_out"][j]))
        wdw = np.asarray(inp["conv_w_dw"][j], np.float32)
        put(("w_dw", j), np.ascontiguousarray(wdw.reshape(CW, NCH, 128).transpose(2, 0, 1).reshape(128, CW * NCH)))
    return v


def make_in_maps(B, inp, ncores):
    x = np.asarray(inp["x"], np.float32)
    meta = np.asarray(inp["meta_tokens"], np.float32)
    vecs = make_vecs(B.VL, inp)
    wqkv = np.asarray(inp["na_w_qkv"], np.float32)
    wqkv_r = np.ascontiguousarray(
        wqkv.reshape(2, NCH, 128, 3, 8, 128).transpose(0, 4, 2, 1, 3, 5)).reshape(2, 8, 128, 3072)
    rpb = np.asarray(inp["na_rpb"], np.float32)
    ck = np.arange(64)[:, None]
    cq = np.arange(64)[None, :]
    dcix = np.clip(ck - cq, -15, 15) + 15
    ii = np.arange(15)
    g = rpb[:, :, 14 - ii][:, :, :, dcix]
    g = g.reshape(2, 8, 2, 15, 64, 64).transpose(0, 1, 2, 4, 3, 5)
    na_bias = np.ascontiguousarray(g).reshape(2, 8, 128, 960)
    sc = np.clip(cq - 8, 0, 48)
    valid = (ck >= sc) & (ck < sc + 16)
    nac = np.zeros((128, 65), np.float32)
    nac[:, 0:64] = np.tile(np.where(valid, 0.0, NEG).astype(np.float32), (2, 1))
    nac[:, 64] = np.tile(np.where(np.arange(64) >= PAD, 0.0, NEG).astype(np.float32), 2)
    shared = {
        "wqkv_r": wqkv_r,
        "na_w_o": np.ascontiguousarray(inp["na_w_o"], dtype=np.float32),
        "na_bias": na_bias,
        "nac": nac,
        "vecs": vecs,
        "ident": np.eye(128, dtype=np.float32),
        "mlp_w1": np.ascontiguousarray(inp["mlp_w1"], dtype=np.float32),
        "mlp_w2": np.ascontiguousarray(inp["mlp_w2"], dtype=np.float32),
        "conv_w_in": np.ascontiguousarray(inp["conv_w_in"], dtype=np.float32),
        "conv_w_out": np.ascontiguousarray(inp["conv_w_out"], dtype=np.float32),
    }
    maps = []
    for b in range(ncores):
        h0 = np.zeros((LT, D), np.float32)
        h0[PAD:PAD + NMETA] = meta
        h0[64:] = x[b]
        m = dict(shared)
        m["h0T"] = np.ascontiguousarray(h0.T)
        maps.append(m)
    return maps


def kernel(**inp):
    B = Builder()
    nc = B.build()
    ncores = 8
    maps = make_in_maps(B, inp, ncores)
    res = run_bass_kernel_spmd(nc, maps, core_ids=list(range(ncores)))
    out = np.stack([np.ascontiguousarray(res.results[b]["outT"].T) for b in range(ncores)], axis=0)
    return out.astype(np.float32)
```

```python
import contextlib
import numpy as np
import concourse.bass as bass
import concourse.mybir as mybir
from concourse.bass_utils import run_bass_kernel_spmd

F32 = mybir.dt.float32
BF16 = mybir.dt.bfloat16
AF = mybir.ActivationFunctionType
ALU = mybir.AluOpType

D = 1024
NCH = 8
DFF = 4096
SEQ = 4096
NMETA = 16
PAD = 48
LT = 64 + SEQ
DEPTH = 4
EPS = 1e-6
CW = 31
NEG = -30000.0
TILES = [(PAD, NMETA)] + [(64 + 512 * i, 512) for i in range(8)]
MLP_TILES = [(PAD, 264), (PAD + 264, 264)] + [(576 + 512 * i, 512) for i in range(7)]


class Sched:
    NDMA = 8

    def __init__(self, nc, es):
        self.nc = nc
        self.engs = {"pe": nc.tensor, "dve": nc.vector, "act": nc.scalar, "pool": nc.gpsimd, "sp": nc.sync}
        self.sem = {e: es.enter_context(nc.semaphore("s_" + e)) for e in self.engs}
        self.cnt = {e: 0 for e in self.engs}
        self.semobj = {("e", e): self.sem[e] for e in self.engs}
        self.dsem = {}
        self.dval = {}
        self.dnext = {}
        for q in ("sp", "pool"):
            self.dnext[q] = 0
            for i in range(self.NDMA):
                k = ("d", q, i)
                self.semobj[k] = es.enter_context(nc.semaphore("d_%s%d" % (q, i)))
                self.dval[k] = 0
        self.seen = {e: {} for e in self.engs}
        self.last_w = {}
        self.readers = {}
        self.nwaits = 0

    def _wait(self, e, sk, v):
        if self.seen[e].get(sk, 0) >= v:
            return
        self.engs[e].wait_ge(self.semobj[sk], v)
        self.seen[e][sk] = v
        self.nwaits += 1

    def _deps(self, e, reads, writes):
        own = ("e", e)
        deps = {}

        def add(sk, v):
            if deps.get(sk, 0) < v:
                deps[sk] = v
        for k in reads:
            w = self.last_w.get(k)
            if w is not None:
                add(*w)
        for k in writes:
            w = self.last_w.get(k)
            if w is not None and w[0] != own:
                add(*w)
            for sk, v in self.readers.get(k, {}).items():
                if sk != own:
                    add(sk, v)
        for sk, v in deps.items():
            self._wait(e, sk, v)

    def _record(self, tok, reads, writes):
        sk, v = tok
        for k in writes:
            self.last_w[k] = tok
            self.readers[k] = {}
        for k in reads:
            r = self.readers.setdefault(k, {})
            if r.get(sk, 0) < v:
                r[sk] = v

    def op(self, e, reads, writes, emit):
        self._deps(e, reads, writes)
        inst = emit(self.engs[e])
        self.cnt[e] += 1
        inst.then_inc(self.sem[e], 1)
        tok = (("e", e), self.cnt[e])
        self._record(tok, reads, writes)
        return tok

    def dma(self, q, out, in_, reads, writes):
        self._deps(q, reads, writes)
        i = self.dnext[q] % self.NDMA
        self.dnext[q] += 1
        k = ("d", q, i)
        self._wait(q, k, self.dval[k])
        self.engs[q].dma_start(out=out, in_=in_).then_inc(self.semobj[k], 16)
        self.dval[k] += 16
        tok = (k, self.dval[k])
        self._record(tok, reads, writes)
        return tok

    def barrier(self):
        for e in self.engs:
            for f in self.engs:
                if f != e and self.cnt[f] > 0:
                    self._wait(e, ("e", f), self.cnt[f])
            for k, v in self.dval.items():
                if v > 0:
                    self._wait(e, k, v)


def _pc(v):
    return np.ascontiguousarray(np.asarray(v, np.float32).reshape(NCH, 128).T)


class VecLayout:
    def __init__(self):
        self.cols = {}
        self.n = 0

    def add(self, name, ncols):
        self.cols[name] = (self.n, ncols)
        self.n += ncols


def vec_layout():
    L = VecLayout()
    for i in range(DEPTH):
        L.add(("mix_g", i), 8)
        L.add(("mlp_g", i), 8)
    L.add("final_g", 8)
    for j in range(2):
        L.add(("b_in", j), 16)
        L.add(("b_dw", j), 8)
        L.add(("ln_g", j), 8)
        L.add(("ln_b", j), 8)
        L.add(("b_out", j), 8)
        L.add(("w_dw", j), CW * 8)
    return L


class Builder:
    def __init__(self, layers=None, do_final=True):
        self.layers = list(range(DEPTH)) if layers is None else layers
        self.do_final = do_final
        self.VL = vec_layout()

    def build(self):
        nc = bass.Bass("TRN2", target_bir_lowering=False)
        self.nc = nc
        es = contextlib.ExitStack()
        self.es = es
        S = Sched(nc, es)
        self.S = S
        dt = nc.dram_tensor
        self.h0T = dt("h0T", [D, LT], F32, kind="ExternalInput").ap()
        self.vecs_d = dt("vecs", [128, self.VL.n], F32, kind="ExternalInput").ap()
        self.mlp_w1 = dt("mlp_w1", [DEPTH, D, DFF], F32, kind="ExternalInput").ap()
        self.mlp_w2 = dt("mlp_w2", [DEPTH, DFF, D], F32, kind="ExternalInput").ap()
        self.conv_w_in = dt("conv_w_in", [2, D, 2 * D], F32, kind="ExternalInput").ap()
        self.conv_w_out = dt("conv_w_out", [2, D, D], F32, kind="ExternalInput").ap()
        self.ident_d = dt("ident", [128, 128], F32, kind="ExternalInput").ap()
        self.wqkv_r = dt("wqkv_r", [2, 8, 128, 3072], F32, kind="ExternalInput").ap()
        self.na_w_o = dt("na_w_o", [2, D, D], F32, kind="ExternalInput").ap()
        self.na_bias = dt("na_bias", [2, 8, 128, 960], F32, kind="ExternalInput").ap()
        self.nac_d = dt("nac", [128, 65], F32, kind="ExternalInput").ap()
        self.outT = dt("outT", [D, SEQ], F32, kind="ExternalOutput").ap()

        sb = lambda name, shape, dtype: es.enter_context(nc.sbuf_tensor(name, shape, dtype))
        self.hT = sb("hT", [128, NCH, LT], F32)
        self.vecs = sb("vecs_sb", [128, self.VL.n], F32)
        self.ones = sb("ones", [128, 128], BF16)
        self.ident = sb("ident_sb", [128, 128], BF16)
        self.ps = [es.enter_context(nc.psum_tensor("ps%d" % i, [128, 512], F32)) for i in range(8)]

        S.op("dve", [], ["ones"], lambda e: e.memset(self.ones[:], 1.0))
        S.dma("sp", self.vecs[:], self.vecs_d, [], ["vecs"])
        S.dma("pool", self.ident[:], self.ident_d, [], ["ident"])
        h0v = self.h0T.rearrange("(c p) t -> p c t", p=128)
        for (a_, b_) in [(0, 576)] + [(576 + 512 * k, 576 + 512 * (k + 1)) for k in range(7)]:
            S.dma("sp", self.hT[:, :, a_:b_], h0v[:, :, a_:b_], [],
                  [k_ for c in range(NCH) for k_ in self.hkeys(c, a_, b_)])

        for i in self.layers:
            if i % 2 == 0:
                self.conv_layer(i)
            else:
                self.na_layer(i)
            self.mlp_layer(i)
        if self.do_final:
            self.final()
        S.barrier()
        es.close()
        return nc

    def vcol(self, name, c=0, n=1):
        o, _ = self.VL.cols[name]
        return self.vecs[:, o + c:o + c + n]

    def hkeys(self, c, a, b):
        return [("h", c, k) for k in range(a // 16, (b + 15) // 16)]

    def rmsnorm_cols(self, c0, n, gname, hn, hn_off, hnkey, W):
        S = self.S
        hT = self.hT
        ssb = W["ps_ss"]
        for c in range(NCH):
            j = W["sq_i"] % 2
            W["sq_i"] += 1
            sq = W["sq"][j]
            S.op("act", self.hkeys(c, c0, c0 + n), [("sq", j)],
                 lambda e, c=c, sq=sq: e.activation(out=sq[:, :n], in_=hT[:, c, c0:c0 + n], func=AF.Square))
            S.op("pe", [("sq", j), "ones"], [("ps", ssb)],
                 lambda e, c=c, sq=sq: e.matmul(self.ps[ssb][:, :n], lhsT=self.ones[:], rhs=sq[:, :n],
                                                start=(c == 0), stop=(c == NCH - 1)))
        rt = W["rt"]
        S.op("act", [("ps", ssb), "epsc"], ["rt"],
             lambda e: e.activation(out=rt[:, :n], in_=self.ps[ssb][:, :n], func=AF.Sqrt,
                                    scale=1.0 / D, bias=W["eps"][:, 0:1]))
        S.op("dve", ["rt"], ["rstd"], lambda e: e.reciprocal(out=W["rstd"][:, :n], in_=rt[:, :n]))
        for c in range(NCH):
            S.op("dve", self.hkeys(c, c0, c0 + n) + ["rstd", "vecs"], [hnkey],
                 lambda e, c=c: e.scalar_tensor_tensor(
                     out=hn[:, c, hn_off:hn_off + n], in0=hT[:, c, c0:c0 + n],
                     scalar=self.vcol(gname, c), in1=W["rstd"][:, :n], op0=ALU.mult, op1=ALU.mult))

    def rmsnorm_tile(self, t, gname, hn, hn_off, hnkey, W):
        c0, n = TILES[t]
        self.rmsnorm_cols(c0, n, gname, hn, hn_off, hnkey, W)

    def norm_work(self, sbt, nmax=512):
        W = {"sq": [sbt("sq%d" % j, [128, nmax], BF16) for j in range(2)], "sq_i": 0,
             "rt": sbt("rt", [128, nmax], F32), "rstd": sbt("rstd", [128, nmax], F32),
             "eps": sbt("epsc", [128, 1], F32), "ps_ss": 7}
        self.S.op("dve", [], ["epsc"], lambda e: e.memset(W["eps"][:], EPS))
        return W

    def mlp_layer(self, i):
        S = self.S
        nc = self.nc
        S.barrier()
        with contextlib.ExitStack() as ls:
            sbt = lambda name, shape, dtype: ls.enter_context(nc.sbuf_tensor("m%d_%s" % (i, name), shape, dtype))
            W = self.norm_work(sbt)
            NSLOT = 3
            hn = sbt("hn", [128, NCH, NSLOT * 512], BF16)
            w1 = [sbt("w1_%d" % j, [128, NCH, 512], BF16) for j in range(2)]
            w2 = [sbt("w2_%d" % j, [128, 4, D], BF16) for j in range(2)]
            ut = [sbt("ut%d" % j, [128, 4, 512], BF16) for j in range(2)]
            rl = [sbt("rl%d" % j, [128, 512], F32) for j in range(2)]
            blocks = [[0, 1, 2], [3, 4, 5], [6, 7, 8]]
            w1d = self.mlp_w1[i].rearrange("(kc p) f -> p kc f", p=128)
            w2d = self.mlp_w2[i].rearrange("(fc p) o -> p fc o", p=128)
            gi = 0

            def load_w(g, j):
                S.dma("pool", w1[j][:], w1d[:, :, g * 512:(g + 1) * 512], [], [("w1", j)])
                S.dma("pool", w2[j][:], w2d[:, g * 4:(g + 1) * 4, :], [], [("w2", j)])

            seq = [(b, g) for b in range(len(blocks)) for g in range(8)]
            load_w(seq[0][1], 0)
            hid_i = 0
            out_i = 0
            ut_i = 0
            for si, (b, g) in enumerate(seq):
                blk = blocks[b]
                if g == 0:
                    for s, t in enumerate(blk):
                        self.rmsnorm_cols(MLP_TILES[t][0], MLP_TILES[t][1], ("mlp_g", i), hn, s * 512, ("hn", s), W)
                j = si % 2
                if si + 1 < len(seq):
                    load_w(seq[si + 1][1], (si + 1) % 2)

                def hid(s, t, uj):
                    nonlocal hid_i
                    c0, n = MLP_TILES[t]
                    for fc in range(4):
                        pb = hid_i % 2
                        hid_i += 1
                        def mm(e, fc=fc, pb=pb):
                            last = None
                            for kc in range(NCH):
                                last = e.matmul(self.ps[pb][:, :n], lhsT=w1[j][:, kc, fc * 128:(fc + 1) * 128],
                                                rhs=hn[:, kc, s * 512:s * 512 + n], start=(kc == 0), stop=(kc == NCH - 1))
                            return last
                        S.op("pe", [("w1", j), ("hn", s)], [("ps", pb)], mm)
                        S.op("act", [("ps", pb)], [("rl", pb)],
                             lambda e, pb=pb: e.activation(out=rl[pb][:, :n], in_=self.ps[pb][:, :n], func=AF.Relu))
                        S.op("act", [("rl", pb)], [("ut", uj, fc)],
                             lambda e, pb=pb, fc=fc: e.activation(out=ut[uj][:, fc, :n], in_=rl[pb][:, :n], func=AF.Square))

                def outp(s, t, uj):
                    nonlocal out_i
                    c0, n = MLP_TILES[t]
                    for oc in range(NCH):
                        pb = 2 + out_i % 4
                        out_i += 1
                        def mm(e, oc=oc, pb=pb):
                            last = None
                            for fc in range(4):
                                last = e.matmul(self.ps[pb][:, :n], lhsT=w2[j][:, fc, oc * 128:(oc + 1) * 128],
                                                rhs=ut[uj][:, fc, :n], start=(fc == 0), stop=(fc == 3))
                            return last
                        S.op("pe", [("w2", j)] + [("ut", uj, fc) for fc in range(4)], [("ps", pb)], mm)
                        S.op("dve", [("ps", pb)] + self.hkeys(oc, c0, c0 + n), self.hkeys(oc, c0, c0 + n),
                             lambda e, oc=oc, pb=pb: e.tensor_tensor(
                                 out=self.hT[:, oc, c0:c0 + n], in0=self.ps[pb][:, :n],
                                 in1=self.hT[:, oc, c0:c0 + n], op=ALU.add))

                pend = None
                for s, t in enumerate(blk):
                    uj = ut_i % 2
                    ut_i += 1
                    hid(s, t, uj)
                    if pend is not None:
                        outp(*pend)
                    pend = (s, t, uj)
                outp(*pend)

    def rmsnorm_gen(self, c0, n, gname, hn, hn_off, hnkey, W):
        S = self.S
        hT = self.hT
        ssb = W["ps_ss"]
        for c in range(NCH):
            j = W["sq_i"] % 2
            W["sq_i"] += 1
            sq = W["sq"][j]
            S.op("act", self.hkeys(c, c0, c0 + n), [("sq", j)],
                 lambda e, c=c, sq=sq: e.activation(out=sq[:, :n], in_=hT[:, c, c0:c0 + n], func=AF.Square))
            S.op("pe", [("sq", j), "ones"], [("ps", ssb)],
                 lambda e, c=c, sq=sq: e.matmul(self.ps[ssb][:, :n], lhsT=self.ones[:], rhs=sq[:, :n],
                                                start=(c == 0), stop=(c == NCH - 1)))
            yield
        rt = W["rt"]
        S.op("act", [("ps", ssb), "epsc"], ["rt"],
             lambda e: e.activation(out=rt[:, :n], in_=self.ps[ssb][:, :n], func=AF.Sqrt,
                                    scale=1.0 / D, bias=W["eps"][:, 0:1]))
        S.op("dve", ["rt"], ["rstd"], lambda e: e.reciprocal(out=W["rstd"][:, :n], in_=rt[:, :n]))
        yield
        for c in range(NCH):
            S.op("dve", self.hkeys(c, c0, c0 + n) + ["rstd", "vecs"], [hnkey],
                 lambda e, c=c: e.scalar_tensor_tensor(
                     out=hn[:, c, hn_off:hn_off + n], in0=hT[:, c, c0:c0 + n],
                     scalar=self.vcol(gname, c), in1=W["rstd"][:, :n], op0=ALU.mult, op1=ALU.mult))
            yield

    @staticmethod
    def interleave(*gens, ratio=None):
        gens = [g for g in gens if g is not None]
        alive = list(gens)
        while alive:
            for g in list(alive):
                try:
                    next(g)
                except StopIteration:
                    alive.remove(g)

    def conv_layer(self, i):
        S = self.S
        nc = self.nc
        jl = i // 2
        S.barrier()
        N = 256
        with contextlib.ExitStack() as ls:
            sbt = lambda name, shape, dtype: ls.enter_context(nc.sbuf_tensor("c%d_%s" % (i, name), shape, dtype))
            W = self.norm_work(sbt, N)
            W["ps_ss"] = 3
            Gb = sbt("Gb", [128, NCH, 30 + N], BF16)
            ctmp = sbt("ctmp", [128, NCH, 30], BF16)
            hn = sbt("hn", [128, NCH, N], BF16)
            sT = sbt("sT", [128, NCH, N], BF16)
            y = sbt("y", [128, NCH, N], F32)
            ybf = [sbt("ybf%d" % k, [128, N], BF16) for k in range(2)]
            ysq = [sbt("ysq%d" % k, [128, N], BF16) for k in range(2)]
            dg = [sbt("dg%d" % k, [128, 8, 128], BF16) for k in range(4)]
            mu = sbt("mu", [128, N], F32)
            vb = sbt("vb", [128, N], F32)
            rs = sbt("rs", [128, N], F32)
            nm = sbt("nm", [128, N], F32)
            sg = [sbt("sg%d" % k, [128, N], F32) for k in range(2)]
            sgz = [sbt("sgz%d" % k, [128, N], F32) for k in range(2)]
            NWB = 4
            wb = [sbt("wb%d" % k, [128, NCH, 512], BF16) for k in range(NWB)]
            win = self.conv_w_in[jl].rearrange("(kc p) f -> p kc f", p=128)
            wout = self.conv_w_out[jl].rearrange("(kc p) f -> p kc f", p=128)
            steps = [(PAD + 242 * k, 242) for k in range(16)] + [(PAD + 242 * 16, 240)]
            assert steps[-1][0] + steps[-1][1] == LT
            nreal = len(steps)
            LAST = len(steps) - 1
            inl = [win[:, :, 0:512], win[:, :, 1024:1536], win[:, :, 512:1024], win[:, :, 1536:2048]]
            outl = [wout[:, :, 0:512], wout[:, :, 512:1024]]
            loads = list(inl)
            for st in range(len(steps)):
                if st >= 1:
                    loads += outl
                if st + 1 < nreal:
                    loads += inl
            loads += outl
            issued = [0]
            consumed = [0]

            def ensure(idx):
                while issued[0] <= min(idx, len(loads) - 1):
                    k = issued[0]
                    S.dma("pool", wb[k % NWB][:], loads[k], [], [("wb", k % NWB)])
                    issued[0] += 1
            lptr = [0]

            def next_w():
                k = lptr[0]
                lptr[0] += 1
                ensure(k)
                return k % NWB

            def done_w(nloads):
                consumed[0] += nloads
                ensure(consumed[0] + NWB - 1)

            S.op("dve", [], [("G", c) for c in range(NCH)], lambda e: e.memset(Gb[:, :, 0:30], 0.0))
            cs = {"inb": 0, "obk": 0, "yi": 0}

            def geom(st):
                c0, n = steps[st]
                m0 = 15 if st == 0 else 0
                nn = n - m0 + (15 if st == LAST else 0)
                a = c0 - 15 + m0
                return c0, n, m0, nn, a, a + nn

            def rms_gen(st):
                if st < nreal:
                    c0, n = steps[st]
                    yield from self.rmsnorm_gen(c0, n, ("mix_g", i), hn, 0, "hnc", W)

            def inproj_gen(st):
                c0, n, m0, nn, a, b = geom(st)
                if st == LAST:
                    S.op("dve", [], [("G", c) for c in range(NCH)],
                         lambda e: e.memset(Gb[:, :, 30 + n:30 + n + 16], 0.0))
                for hf in range(2):
                    ba = next_w()
                    bg = next_w()
                    for q in range(4):
                        oc = hf * 4 + q
                        pa = cs["inb"] % 4
                        cs["inb"] += 1
                        def mmag(e, q=q, pa=pa, ba=ba, bg=bg):
                            last = None
                            for kc in range(NCH):
                                last = e.matmul(self.ps[pa][:, 0:n], lhsT=wb[ba][:, kc, q * 128:(q + 1) * 128],
                                                rhs=hn[:, kc, :n], start=(kc == 0), stop=(kc == NCH - 1))
                            for kc in range(NCH):
                                last = e.matmul(self.ps[pa][:, 256:256 + n], lhsT=wb[bg][:, kc, q * 128:(q + 1) * 128],
                                                rhs=hn[:, kc, :n], start=(kc == 0), stop=(kc == NCH - 1))
                            return last
                        S.op("pe", [("wb", ba), ("wb", bg), "hnc"], [("ps", pa)], mmag)
                        sgi = cs["inb"] % 2
                        S.op("act", [("ps", pa), "vecs"], [("sg", sgi)],
                             lambda e, pa=pa, oc=oc, sgi=sgi: e.activation(
                                 out=sg[sgi][:, :n], in_=self.ps[pa][:, 256:256 + n], func=AF.Sigmoid,
                                 bias=self.vcol(("b_in", jl), 8 + oc)))
                        S.op("dve", [("ps", pa), ("sg", sgi), "vecs"], [("G", oc)],
                             lambda e, pa=pa, oc=oc, sgi=sgi: e.scalar_tensor_tensor(
                                 out=Gb[:, oc, 30:30 + n], in0=self.ps[pa][:, 0:n],
                                 scalar=self.vcol(("b_in", jl), oc), in1=sg[sgi][:, :n],
                                 op0=ALU.add, op1=ALU.mult))
                        yield
                    done_w(2)

            pending = [None]

            def builds(c):
                for qt in range(4):
                    for k in range(8 * qt, min(8 * qt + 8, CW)):
                        wcol = self.vcol(("w_dw", jl), k * 8 + c)
                        dkey = ("dg", qt, k % 8)
                        if k % 8 in (3, 7):
                            S.op("act", ["ident", "vecs"], [dkey],
                                 lambda e, k=k, qt=qt, wcol=wcol: e.activation(
                                     out=dg[qt][:, k % 8, :], in_=self.ident[:], func=AF.Copy, scale=wcol))
                        elif k % 8 in (1, 5):
                            S.op("pool", ["ident", "vecs"], [dkey],
                                 lambda e, k=k, qt=qt, wcol=wcol: e.tensor_scalar(
                                     out=dg[qt][:, k % 8, :], in0=self.ident[:], scalar1=wcol, scalar2=0.0,
                                     op0=ALU.mult, op1=ALU.add))
                        else:
                            S.op("dve", ["ident", "vecs"], [dkey],
                                 lambda e, k=k, qt=qt, wcol=wcol: e.tensor_scalar(
                                     out=dg[qt][:, k % 8, :], in0=self.ident[:], scalar1=wcol, scalar2=None,
                                     op0=ALU.mult))

            def conv_gen(st):
                c0, n, m0, nn, a, b = geom(st)
                for c in range(NCH):
                    for qt in range(4):
                        ks = list(range(8 * qt, min(8 * qt + 8, CW)))
                        pb = 4 + c % 2
                        def mmc(e, c=c, ks=ks, qt=qt, pb=pb):
                            last = None
                            for k in ks:
                                last = e.matmul(self.ps[pb][:, :nn], lhsT=dg[qt][:, k % 8, :],
                                                rhs=Gb[:, c, m0 + k:m0 + k + nn], start=(k == 0), stop=(k == CW - 1))
                            return last
                        S.op("pe", [("dg", qt, k % 8) for k in ks] + [("G", c)], [("ps", pb)], mmc)
                    if not (st == len(steps) - 1 and c == NCH - 1):
                        builds((c + 1) % NCH)
                    if pending[0] is not None:
                        pending[0]()
                        pending[0] = None
                    pb = 4 + c % 2
                    bdw = self.vcol(("b_dw", jl), c)
                    yj = cs["yi"] % 2
                    cs["yi"] += 1
                    S.op("act", [("ps", pb), "vecs"], [("y", c)],
                         lambda e, c=c, pb=pb, bdw=bdw: e.activation(out=y[:, c, :nn], in_=self.ps[pb][:, :nn],
                                                                     func=AF.Identity, bias=bdw))
                    S.op("act", [("ps", pb), "vecs"], [("ybf", yj)],
                         lambda e, pb=pb, bdw=bdw, yj=yj: e.activation(out=ybf[yj][:, :nn], in_=self.ps[pb][:, :nn],
                                                                       func=AF.Identity, bias=bdw))
                    S.op("act", [("ps", pb), "vecs"], [("ysq", yj)],
                         lambda e, pb=pb, bdw=bdw, yj=yj: e.activation(out=ysq[yj][:, :nn], in_=self.ps[pb][:, :nn],
                                                                       func=AF.Square, bias=bdw))
                    def stats(c=c, yj=yj):
                        S.op("pe", [("ybf", yj), "ones"], [("ps", 6)],
                             lambda e: e.matmul(self.ps[6][:, :nn], lhsT=self.ones[:], rhs=ybf[yj][:, :nn],
                                                start=(c == 0), stop=(c == NCH - 1)))
                        S.op("pe", [("ysq", yj), "ones"], [("ps", 7)],
                             lambda e: e.matmul(self.ps[7][:, :nn], lhsT=self.ones[:], rhs=ysq[yj][:, :nn],
                                                start=(c == 0), stop=(c == NCH - 1)))
                    pending[0] = stats
                    yield
                pending[0]()
                pending[0] = None
                if st != LAST:
                    S.op("dve", [("G", c) for c in range(NCH)], ["ctmp"],
                         lambda e: e.tensor_copy(out=ctmp[:], in_=Gb[:, :, n:n + 30]))
                    S.op("dve", ["ctmp"], [("G", c) for c in range(NCH)],
                         lambda e: e.tensor_copy(out=Gb[:, :, 0:30], in_=ctmp[:]))

            def ln_gen(st):
                c0, n, m0, nn, a, b = geom(st)
                S.op("act", [("ps", 6)], ["mu"],
                     lambda e: e.activation(out=mu[:, :nn], in_=self.ps[6][:, :nn], func=AF.Copy, scale=1.0 / D))
                S.op("dve", ["mu"], ["vb"],
                     lambda e: e.tensor_tensor(out=vb[:, :nn], in0=mu[:, :nn], in1=mu[:, :nn], op=ALU.mult))
                S.op("dve", [("ps", 7), "vb"], ["vb"],
                     lambda e: e.scalar_tensor_tensor(out=vb[:, :nn], in0=self.ps[7][:, :nn], scalar=1.0 / D,
                                                      in1=vb[:, :nn], op0=ALU.mult, op1=ALU.subtract))
                yield
                S.op("act", ["vb", "epsc"], ["vb"],
                     lambda e: e.activation(out=vb[:, :nn], in_=vb[:, :nn], func=AF.Sqrt, bias=W["eps"][:, 0:1]))
                S.op("dve", ["vb"], ["rs"], lambda e: e.reciprocal(out=rs[:, :nn], in_=vb[:, :nn]))
                S.op("dve", ["mu", "rs"], ["nm"],
                     lambda e: e.scalar_tensor_tensor(out=nm[:, :nn], in0=mu[:, :nn], scalar=-1.0, in1=rs[:, :nn],
                                                      op0=ALU.mult, op1=ALU.mult))
                yield

            def ln_chunks_gen(st):
                c0, n, m0, nn, a, b = geom(st)
                for c in range(NCH):
                    S.op("dve", [("y", c), "rs"], [("y", c)],
                         lambda e, c=c: e.tensor_tensor(out=y[:, c, :nn], in0=y[:, c, :nn], in1=rs[:, :nn], op=ALU.mult))
                    S.op("dve", [("y", c), "nm"], [("y", c)],
                         lambda e, c=c: e.tensor_tensor(out=y[:, c, :nn], in0=y[:, c, :nn], in1=nm[:, :nn], op=ALU.add))
                    zi = c % 2
                    S.op("act", [("y", c), "vecs"], [("sgz", zi)],
                         lambda e, c=c, zi=zi: e.activation(out=sgz[zi][:, :nn], in_=y[:, c, :nn], func=AF.Sigmoid,
                                                            scale=self.vcol(("ln_g", jl), c),
                                                            bias=self.vcol(("ln_b", jl), c)))
                    S.op("act", [("y", c), "vecs"], [("y", c)],
                         lambda e, c=c: e.activation(out=y[:, c, :nn], in_=y[:, c, :nn], func=AF.Identity,
                                                     scale=self.vcol(("ln_g", jl), c), bias=self.vcol(("ln_b", jl), c)))
                    S.op("pool", [("y", c), ("sgz", zi)], [("sT", c)],
                         lambda e, c=c, zi=zi: e.tensor_tensor(out=sT[:, c, :nn], in0=y[:, c, :nn],
                                                               in1=sgz[zi][:, :nn], op=ALU.mult))
                    yield

            def outproj_gen(st):
                c0, n, m0, nn, a, b = geom(st)
                for hf in range(2):
                    bo = next_w()
                    for q in range(4):
                        oc = hf * 4 + q
                        pb = cs["obk"] % 4
                        cs["obk"] += 1
                        def mmo(e, q=q, pb=pb, bo=bo):
                            last = None
                            for kc in range(NCH):
                                last = e.matmul(self.ps[pb][:, :nn], lhsT=wb[bo][:, kc, q * 128:(q + 1) * 128],
                                                rhs=sT[:, kc, :nn], start=(kc == 0), stop=(kc == NCH - 1))
                            return last
                        S.op("pe", [("wb", bo)] + [("sT", c) for c in range(NCH)], [("ps", pb)], mmo)
                        hk = self.hkeys(oc, a, b)
                        S.op("dve", [("ps", pb), "vecs"] + hk, hk,
                             lambda e, oc=oc, pb=pb: e.scalar_tensor_tensor(
                                 out=self.hT[:, oc, a:b], in0=self.ps[pb][:, :nn], scalar=self.vcol(("b_out", jl), oc),
                                 in1=self.hT[:, oc, a:b], op0=ALU.add, op1=ALU.add))
                        yield
                    done_w(1)

            def seq(*gens):
                for g in gens:
                    if g is not None:
                        yield from g

            ensure(NWB - 1)
            self.interleave(rms_gen(0))
            builds(0)
            self.interleave(inproj_gen(0))
            for st in range(len(steps)):
                nxt = st + 1 < len(steps)
                cg = conv_gen(st)
                rg = rms_gen(st + 1) if nxt else iter(())
                lc = ln_chunks_gen(st - 1) if st >= 1 else iter(())
                for _c in range(4):
                    next(lc, None)
                    next(lc, None)
                    next(cg)
                    for _k in range(5):
                        next(rg, None)
                for _ in rg:
                    pass
                if st >= 1:
                    for _ in outproj_gen(st - 1):
                        pass
                for _ in cg:
                    pass
                for _ in ln_gen(st):
                    pass
                if nxt:
                    for _ in inproj_gen(st + 1):
                        pass
            for _ in ln_chunks_gen(len(steps) - 1):
                pass
            for _ in outproj_gen(len(steps) - 1):
                pass

    def na_layer(self, i):
        S = self.S
        nc = self.nc
        jl = i // 2
        S.barrier()
        with contextlib.ExitStack() as ls:
            sbt = lambda name, shape, dtype: ls.enter_context(nc.sbuf_tensor("a%d_%s" % (i, name), shape, dtype))
            W = self.norm_work(sbt, 512)
            T = sbt("T", [128, 8, 960], BF16)
            nac = sbt("nac", [128, 65], F32)
            zer = sbt("zer", [128, 128], BF16)
            obd = sbt("obd", [128, 128], BF16)
            S.dma("sp", nac[:], self.nac_d, [], ["nac"])
            S.op("dve", [], ["zer"], lambda e: e.memset(zer[:], 0.0))
            S.op("dve", [], ["obd"], lambda e: e.memset(obd[:], 0.0))
            S.op("dve", ["obd"], ["obd"], lambda e: e.memset(obd[0:64, 0:64], 1.0))
            S.op("dve", ["obd"], ["obd"], lambda e: e.memset(obd[64:128, 64:128], 1.0))
            with contextlib.ExitStack() as ls2:
                stg = [ls2.enter_context(nc.sbuf_tensor("a%d_stg%d" % (i, k), [128, 960], F32)) for k in range(2)]
                for pr in range(8):
                    S.dma("sp", stg[pr % 2][:], self.na_bias[jl, pr], [], [("stg", pr % 2)])
                    for ii in range(15):
                        S.op("dve", [("stg", pr % 2), "nac"], [("T", pr)],
                             lambda e, pr=pr, ii=ii: e.tensor_tensor(
                                 out=T[:, pr, ii * 64:(ii + 1) * 64], in0=stg[pr % 2][:, ii * 64:(ii + 1) * 64],
                                 in1=nac[:, 0:64], op=ALU.add))
                S.barrier()
            hnr = sbt("hnr", [128, NCH, 3 * 512], BF16)
            hnm = sbt("hnm", [128, NCH, 64], BF16)
            wq = [sbt("wq%d" % k, [128, NCH, 384], BF16) for k in range(2)]
            wo = sbt("wo", [128, D], BF16)
            qT = sbt("qT", [128, 512], BF16)
            Kbd = sbt("Kbd", [128, 16, 128], BF16)
            Vbd = sbt("Vbd", [128, 16, 128], BF16)
            PT = [sbt("PT%d" % k, [128, 512], BF16) for k in range(2)]
            OT = sbt("OT", [128, 512], BF16)
            S.op("dve", [], ["Kbd"], lambda e: e.memset(Kbd[:], 0.0))
            S.op("dve", [], [("Vbd", 0, 0), ("Vbd", 0, 1), ("Vbd", 8, 0), ("Vbd", 8, 1)],
                 lambda e: e.memset(Vbd[:], 0.0))
            ps = self.ps

            def dup(ap):
                return bass.AP(ap.tensor, ap.offset, (tuple(ap.ap[0]), (0, 2), (1, 64)))

            def sr(rq):
                return min(max(rq - 4, 0), 56)

            self.rmsnorm_cols(0, 64, ("mix_g", i), hnm, 0, "hnm", W)
            self.rmsnorm_tile(1, ("mix_g", i), hnr, 512, ("hnr", 1), W)
            self.rmsnorm_tile(2, ("mix_g", i), hnr, 1024, ("hnr", 2), W)

            def rowsrc(rk):
                tk = rk // 8 + 1
                return ("hnr", tk % 3), (tk % 3) * 512 + (rk % 8) * 64

            items = [(-1, p) for p in range(8)] + [(j, p) for j in range(8) for p in range(8)]
            wqd = self.wqkv_r[jl]
            lnd = W["rt"]

            def load_wq(idx):
                S.dma("pool", wq[idx % 2][:], wqd[items[idx][1]].rearrange("p (kc f) -> p kc f", kc=NCH),
                      [], [("wq", idx % 2)])

            def load_wo(idx):
                p = items[idx][1]
                S.dma("pool", wo[:], self.na_w_o[jl][p * 128:(p + 1) * 128, :], [], ["wo"])
            cnt = {"pt": 0, "sb": 0, "ob": 0, "kb": 0}
            info = {}

            def proj(idx):
                j, p = items[idx]
                wb_ = wq[idx % 2]
                wk = ("wq", idx % 2)
                if j < 0:
                    N = 64
                    qkey, qoff, qsrc = "hnm", 0, hnm
                    rows = []
                    rqs = []
                else:
                    N = 512
                    tq = j + 1
                    if p == 0 and tq + 1 <= 8 and tq + 1 > 2:
                        self.rmsnorm_tile(tq + 1, ("mix_g", i), hnr, ((tq + 1) % 3) * 512, ("hnr", (tq + 1) % 3), W)
                    qkey, qoff, qsrc = ("hnr", tq % 3), (tq % 3) * 512, hnr
                    rqs = list(range(8 * j, 8 * j + 8))
                    lo = min(sr(r) for r in rqs)
                    hi = max(sr(r) + 7 for r in rqs)
                    rows = list(range(lo, hi + 1))
                nslots = 1 + len(rows)
                info[idx] = (N, rows, rqs, nslots)
                def mmq(e):
                    last = None
                    for kc in range(NCH):
                        last = e.matmul(ps[0][:, :N], lhsT=wb_[:, kc, 0:128], rhs=qsrc[:, kc, qoff:qoff + N],
                                        start=(kc == 0), stop=(kc == NCH - 1))
                    return last
                S.op("pe", [wk, qkey], [("ps", 0)], mmq)
                S.op("act", [("ps", 0)], ["qT"],
                     lambda e: e.activation(out=qT[:, :N], in_=ps[0][:, :N], func=AF.Copy, scale=0.125))
                segs = [("hnm", hnm, 0, 0, 1)]
                r = 0
                while r < len(rows):
                    rk = rows[r]
                    tk = rk // 8 + 1
                    nr = 1
                    while r + nr < len(rows) and (rows[r + nr] // 8 + 1) == tk:
                        nr += 1
                    key, off = rowsrc(rk)
                    segs.append((key, hnr, off, 1 + r, nr))
                    r += nr
                for (key, src, off, s0, nr) in segs:
                    pb = 1 + cnt["kb"] % 2
                    cnt["kb"] += 1
                    ncol = nr * 64
                    def mmk(e, src=src, off=off, ncol=ncol, pb=pb):
                        last = None
                        for kc in range(NCH):
                            last = e.matmul(ps[pb][:, :ncol], lhsT=wb_[:, kc, 128:256], rhs=src[:, kc, off:off + ncol],
                                            start=(kc == 0), stop=(kc == NCH - 1))
                        return last
                    S.op("pe", [wk, key], [("ps", pb)], mmk)
                    S.op("dve", [("ps", pb)], ["Kbd"],
                         lambda e, pb=pb, s0=s0, nr=nr, ncol=ncol: e.tensor_copy(
                             out=Kbd[0:64, s0:s0 + nr, 0:64],
                             in_=ps[pb][0:64, 0:ncol].rearrange("p (r c) -> p r c", c=64)))
                    S.op("act", [("ps", pb)], ["Kbd"],
                         lambda e, pb=pb, s0=s0, nr=nr, ncol=ncol: e.activation(
                             out=Kbd[64:128, s0:s0 + nr, 64:128],
                             in_=ps[pb][64:128, 0:ncol].rearrange("p (r c) -> p r c", c=64), func=AF.Copy))
                vsrc = [("hnm", hnm, 0)] + [(rowsrc(rk)[0], hnr, rowsrc(rk)[1]) for rk in rows]
                for g0 in range(0, nslots, 8):
                    g = min(8, nslots - g0)
                    vb_ = 3 if g0 == 0 else 0
                    keys = list({vsrc[g0 + q][0] for q in range(g)})
                    def mmv(e, g0=g0, g=g, vb_=vb_):
                        last = None
                        for q in range(g):
                            _, src, off = vsrc[g0 + q]
                            for kc in range(NCH):
                                e.matmul(ps[vb_][0:64, q * 64:(q + 1) * 64], lhsT=src[:, kc, off:off + 64],
                                         rhs=wb_[:, kc, 256:320], start=(kc == 0), stop=(kc == NCH - 1))
                                last = e.matmul(ps[vb_][64:128, q * 64:(q + 1) * 64], lhsT=src[:, kc, off:off + 64],
                                                rhs=wb_[:, kc, 320:384], start=(kc == 0), stop=(kc == NCH - 1))
                        return last
                    S.op("pe", [wk] + keys, [("ps", vb_)], mmv)
                    S.op("dve", [("ps", vb_)], [("Vbd", g0, 0)],
                         lambda e, g0=g0, g=g, vb_=vb_: e.tensor_copy(
                             out=Vbd[0:64, g0:g0 + g, 0:64],
                             in_=ps[vb_][0:64, 0:g * 64].rearrange("p (q f) -> p q f", f=64)))
                    S.op("act", [("ps", vb_)], [("Vbd", g0, 1)],
                         lambda e, g0=g0, g=g, vb_=vb_: e.activation(
                             out=Vbd[64:128, g0:g0 + g, 64:128],
                             in_=ps[vb_][64:128, 0:g * 64].rearrange("p (q f) -> p q f", f=64), func=AF.Copy))

            def attention(idx):
                j, p = items[idx]
                N, rows, rqs, nslots = info[idx]
                geo = []
                for s_ in range(nslots):
                    if s_ == 0:
                        geo.append((0, N, None))
                    else:
                        rk = rows[s_ - 1]
                        att = [rq for rq in rqs if sr(rq) <= rk <= sr(rq) + 7]
                        qa, qb = att[0], att[-1]
                        assert att == list(range(qa, qb + 1))
                        i0 = 7 - rk + qa
                        assert 0 <= i0 and i0 + (qb - qa) <= 14
                        geo.append(((qa - 8 * j) * 64, (qb - qa + 1) * 64, i0))
                sbank = {}

                SB = [4, 5, 1]

                def smm(s_):
                    cq0, ncq, i0 = geo[s_]
                    pb = SB[cnt["sb"] % 3]
                    cnt["sb"] += 1
                    sbank[s_] = pb
                    S.op("pe", ["Kbd", "qT"], [("ps", pb)],
                         lambda e: e.matmul(ps[pb][:, :ncq], lhsT=Kbd[:, s_, :], rhs=qT[:, cq0:cq0 + ncq],
                                            start=True, stop=True))
                    if i0 is not None:
                        S.op("dve", [("ps", pb), ("T", p)], [("ps", pb)],
                             lambda e: e.tensor_tensor(out=ps[pb][:, :ncq], in0=ps[pb][:, :ncq],
                                                       in1=T[:, p, i0 * 64:i0 * 64 + ncq], op=ALU.add))
                smm(0)
                if nslots > 1:
                    smm(1)
                for s_ in range(nslots):
                    if s_ + 2 < nslots:
                        smm(s_ + 2)
                    cq0, ncq, i0 = geo[s_]
                    pb = sbank[s_]
                    pj = cnt["pt"] % 2
                    cnt["pt"] += 1
                    if s_ == 0:
                        S.op("act", [("ps", pb), "nac"], [("PT", pj)],
                             lambda e, pb=pb, pj=pj, ncq=ncq: e.activation(out=PT[pj][:, :ncq], in_=ps[pb][:, :ncq],
                                                                           func=AF.Exp, bias=nac[:, 64:65]))
                    else:
                        S.op("act", [("ps", pb)], [("PT", pj)],
                             lambda e, pb=pb, pj=pj, ncq=ncq: e.activation(out=PT[pj][:, :ncq], in_=ps[pb][:, :ncq],
                                                                           func=AF.Exp))
                    lastslot = s_ == nslots - 1
                    def mmpv(e, s_=s_, cq0=cq0, ncq=ncq, pj=pj, lastslot=lastslot):
                        e.matmul(ps[6][:, cq0:cq0 + ncq], lhsT=Vbd[:, s_, :], rhs=PT[pj][:, :ncq],
                                 start=(s_ == 0), stop=lastslot)
                        return e.matmul(ps[7][:, cq0:cq0 + ncq], lhsT=obd[:], rhs=PT[pj][:, :ncq],
                                        start=(s_ == 0), stop=lastslot)
                    vg = (s_ // 8) * 8
                    S.op("pe", [("Vbd", vg, 0), ("Vbd", vg, 1), "obd", ("PT", pj)], [("ps", 6), ("ps", 7)], mmpv)
                S.op("act", [("ps", 7)], ["rt"],
                     lambda e: e.activation(out=lnd[:, :N], in_=ps[7][:, :N], func=AF.Ln))
                S.op("act", ["rt"], ["rt"],
                     lambda e: e.activation(out=lnd[:, :N], in_=lnd[:, :N], func=AF.Exp, scale=-1.0))
                S.op("dve", [("ps", 6), "rt"], ["OT"],
                     lambda e: e.tensor_tensor(out=OT[:, :N], in0=ps[6][:, :N], in1=lnd[:, :N], op=ALU.mult))

            def outproj(idx):
                j, p = items[idx]
                N = info[idx][0]
                for oc in range(NCH):
                    pb = [6, 7, 4, 5][cnt["ob"] % 4]
                    cnt["ob"] += 1
                    S.op("pe", ["wo", "OT"], [("ps", pb)],
                         lambda e, oc=oc, pb=pb: e.matmul(ps[pb][:, :N], lhsT=wo[:, oc * 128:(oc + 1) * 128],
                                                          rhs=OT[:, :N], start=True, stop=True))
                    if j < 0:
                        a_, b_, o0 = PAD, 64, PAD
                    else:
                        a_, b_, o0 = 64 + 512 * j, 64 + 512 * (j + 1), 0
                    hk = self.hkeys(oc, a_, b_)
                    S.op("dve", [("ps", pb)] + hk, hk,
                         lambda e, oc=oc, pb=pb, a_=a_, b_=b_, o0=o0: e.tensor_tensor(
                             out=self.hT[:, oc, a_:b_], in0=ps[pb][:, o0:o0 + (b_ - a_)],
                             in1=self.hT[:, oc, a_:b_], op=ALU.add))

            load_wq(0)
            load_wq(1)
            load_wo(0)
            proj(0)
            for idx in range(len(items)):
                attention(idx)
                if idx + 1 < len(items):
                    proj(idx + 1)
                    if idx + 2 < len(items):
                        load_wq(idx + 2)
                outproj(idx)
                if idx + 1 < len(items):
                    load_wo(idx + 1)

    def final(self):
        S = self.S
        nc = self.nc
        S.barrier()
        with contextlib.ExitStack() as ls:
            sbt = lambda name, shape, dtype: ls.enter_context(nc.sbuf_tensor("f_" + name, shape, dtype))
            W = self.norm_work(sbt)
            ob = [sbt("ob%d" % j, [128, NCH, 512], F32) for j in range(2)]
            for t in range(1, 9):
                j = t % 2
                c0, n = TILES[t]
                self.rmsnorm_tile(t, "final_g", ob[j], 0, ("ob", j), W)
                for c in range(NCH):
                    S.dma("sp", self.outT[c * 128:(c + 1) * 128, c0 - 64:c0 - 64 + n], ob[j][:, c, :n],
                          [("ob", j)], [("out", c, t)])
            S.barrier()


def make_vecs(VL, inp):
    v = np.zeros((128, VL.n), np.float32)

    def put(name, arr):
        o, n = VL.cols[name]
        assert arr.shape == (128, n), (name, arr.shape, n)
        v[:, o:o + n] = arr
    for i in range(DEPTH):
        put(("mix_g", i), _pc(inp["norm_mix_g"][i]))
        put(("mlp_g", i), _pc(inp["norm_mlp_g"][i]))
    put("final_g", _pc(inp["final_norm_g"]))
    for j in range(2):
        b_in = np.asarray(inp["conv_b_in"][j], np.float32)
        put(("b_in", j), np.ascontiguousarray(b_in.reshape(16, 128).T))
        put(("b_dw", j), _pc(inp["conv_b_dw"][j]))
        put(("ln_g", j), _pc(inp["conv_ln_g"][j]))
        put(("ln_b", j), _pc(inp["conv_ln_b"][j]))
        put(("b_out", j), _pc(inp["conv_b_out"][j]))
        wdw = np.asarray(inp["conv_w_dw"][j], np.float32)
        put(("w_dw", j), np.ascontiguousarray(wdw.reshape(CW, NCH, 128).transpose(2, 0, 1).reshape(128, CW * NCH)))
    return v


def make_in_maps(B, inp, ncores):
    x = np.asarray(inp["x"], np.float32)
    meta = np.asarray(inp["meta_tokens"], np.float32)
    vecs = make_vecs(B.VL, inp)
    wqkv = np.asarray(inp["na_w_qkv"], np.float32)
    wqkv_r = np.ascontiguousarray(
        wqkv.reshape(2, NCH, 128, 3, 8, 128).transpose(0, 4, 2, 1, 3, 5)).reshape(2, 8, 128, 3072)
    rpb = np.asarray(inp["na_rpb"], np.float32)
    ck = np.arange(64)[:, None]
    cq = np.arange(64)[None, :]
    dcix = np.clip(ck - cq, -15, 15) + 15
    ii = np.arange(15)
    g = rpb[:, :, 14 - ii][:, :, :, dcix]
    g = g.reshape(2, 8, 2, 15, 64, 64).transpose(0, 1, 2, 4, 3, 5)
    na_bias = np.ascontiguousarray(g).reshape(2, 8, 128, 960)
    sc = np.clip(cq - 8, 0, 48)
    valid = (ck >= sc) & (ck < sc + 16)
    nac = np.zeros((128, 65), np.float32)
    nac[:, 0:64] = np.tile(np.where(valid, 0.0, NEG).astype(np.float32), (2, 1))
    nac[:, 64] = np.tile(np.where(np.arange(64) >= PAD, 0.0, NEG).astype(np.float32), 2)
    shared = {
        "wqkv_r": wqkv_r,
        "na_w_o": np.ascontiguousarray(inp["na_w_o"], dtype=np.float32),
        "na_bias": na_bias,
        "nac": nac,
        "vecs": vecs,
        "ident": np.eye(128, dtype=np.float32),
        "mlp_w1": np.ascontiguousarray(inp["mlp_w1"], dtype=np.float32),
        "mlp_w2": np.ascontiguousarray(inp["mlp_w2"], dtype=np.float32),
        "conv_w_in": np.ascontiguousarray(inp["conv_w_in"], dtype=np.float32),
        "conv_w_out": np.ascontiguousarray(inp["conv_w_out"], dtype=np.float32),
    }
    maps = []
    for b in range(ncores):
        h0 = np.zeros((LT, D), np.float32)
        h0[PAD:PAD + NMETA] = meta
        h0[64:] = x[b]
        m = dict(shared)
        m["h0T"] = np.ascontiguousarray(h0.T)
        maps.append(m)
    return maps


def kernel(**inp):
    B = Builder()
    nc = B.build()
    ncores = 8
    maps = make_in_maps(B, inp, ncores)
    res = run_bass_kernel_spmd(nc, maps, core_ids=list(range(ncores)))
    out = np.stack([np.ascontiguousarray(res.results[b]["outT"].T) for b in range(ncores)], axis=0)
    return out.astype(np.float32)
```
